# Optimizing a Trainium2 kernel written in Bass

```python
import jax, jax.numpy as jnp
from jax import lax
import numpy as np

D_MODEL = 1024
BATCH = 16
SEQ = 2048
DEPTH = 1

MEM_LEN = 256
MEM_HEADS = 4
MEM_DIM = 128
MLA_HEADS = 8
MLA_NOPE = 64
MLA_ROPE = 32
MLA_V = 64
Q_LORA = 384
KV_LORA = 256
ROPE_THETA = 10000.0
SB_HEADS = 8
SB_DIM = 64
D_FF = 4 * D_MODEL
N_BRANCH = 3
BRANCH_W = 512
Q_BLOCK = 128
EPS = 1e-6

IN_SIZES = [Q_LORA, KV_LORA, MLA_ROPE, 3 * SB_HEADS * SB_DIM, MEM_HEADS * MEM_DIM, N_BRANCH * D_MODEL]
IN_WIDTH = int(sum(IN_SIZES))
IN_SPLITS = [int(v) for v in np.cumsum(IN_SIZES)[:-1]]

kernel_name = "hybrid_mla_stickbreak_memxattn_gated"


def rms_norm(x, g):
    xf = x.astype(jnp.float32)
    y = xf * lax.rsqrt(jnp.mean(xf * xf, axis=-1, keepdims=True) + EPS)
    return (y * g.astype(jnp.float32)).astype(x.dtype)


def apply_rope(x, positions):
    half = MLA_ROPE // 2
    inv_freq = 1.0 / (ROPE_THETA ** (jnp.arange(half, dtype=jnp.float32) * (2.0 / MLA_ROPE)))
    ang = positions.astype(jnp.float32)[..., None] * inv_freq
    cos = jnp.cos(ang)[:, :, None, :]
    sin = jnp.sin(ang)[:, :, None, :]
    xf = x.astype(jnp.float32)
    x1, x2 = xf[..., :half], xf[..., half:]
    out = jnp.concatenate([x1 * cos - x2 * sin, x1 * sin + x2 * cos], axis=-1)
    return out.astype(x.dtype)


def causal_softmax_attention(q, k, v, scale):
    b, s, h, _ = q.shape
    outs = []
    for i in range(s // Q_BLOCK):
        q0, q1 = i * Q_BLOCK, (i + 1) * Q_BLOCK
        qb, kb, vb = q[:, q0:q1], k[:, :q1], v[:, :q1]
        sc = jnp.einsum('bqhd,bkhd->bhqk', qb, kb).astype(jnp.float32) * scale
        t_idx = jnp.arange(q0, q1)[:, None]
        s_idx = jnp.arange(q1)[None, :]
        sc = jnp.where(t_idx >= s_idx, sc, jnp.finfo(jnp.float32).min)
        p = jax.nn.softmax(sc, axis=-1).astype(v.dtype)
        outs.append(jnp.einsum('bhqk,bkhd->bqhd', p, vb))
    o = jnp.concatenate(outs, axis=1)
    return o.reshape(b, s, h * v.shape[-1])


def stick_breaking_attention(q, k, v, scale):
    b, s, h, _ = q.shape
    outs = []
    for i in range(s // Q_BLOCK):
        q0, q1 = i * Q_BLOCK, (i + 1) * Q_BLOCK
        qb, kb, vb = q[:, q0:q1], k[:, :q1], v[:, :q1]
        z = jnp.einsum('bqhd,bkhd->bhqk', qb, kb).astype(jnp.float32) * scale
        t_idx = jnp.arange(q0, q1)[:, None]
        s_idx = jnp.arange(q1)[None, :]
        strict = t_idx > s_idx
        log_keep = jnp.where(strict, jax.nn.log_sigmoid(-z), 0.0)
        rev_incl = lax.cumsum(log_keep, axis=3, reverse=True)
        rev_excl = jnp.concatenate([rev_incl[..., 1:], jnp.zeros_like(rev_incl[..., :1])], axis=-1)
        log_a = jax.nn.log_sigmoid(z) + rev_excl
        a = jnp.where(strict, jnp.exp(log_a), 0.0).astype(v.dtype)
        outs.append(jnp.einsum('bhqk,bkhd->bqhd', a, vb))
    o = jnp.concatenate(outs, axis=1)
    return o.reshape(b, s, h * v.shape[-1])


def memory_cross_attention(q, k, v, scale):
    b, s, h, d = q.shape
    sc = jnp.einsum('bshd,bmhd->bhsm', q, k).astype(jnp.float32) * scale
    p = jax.nn.softmax(sc, axis=-1).astype(v.dtype)
    return jnp.einsum('bhsm,bmhd->bshd', p, v).reshape(b, s, h * d)


def setup_inputs(seed: int = 0) -> dict:
    key = jax.random.key(seed)
    ks = jax.random.split(key, 24)
    f32 = jnp.float32

    def w(k, shape, fan_in):
        return jax.random.normal(k, shape, f32) * (fan_in ** -0.5)

    def gain(k, shape):
        return 1.0 + 0.02 * jax.random.normal(k, shape, f32)

    x = jax.random.normal(ks[0], (BATCH, SEQ, D_MODEL), f32)
    mem = jax.random.normal(ks[1], (BATCH, MEM_LEN, D_MODEL), f32)
    offsets = jax.random.randint(ks[2], (BATCH, 1), 0, 4096, dtype=jnp.int32)
    positions = (jnp.arange(SEQ, dtype=jnp.int32)[None, :] + offsets).astype(jnp.int32)
    return {
        "x": x,
        "mem": mem,
        "positions": positions,
        "ln_mix_pre": gain(ks[3], (DEPTH, D_MODEL)),
        "w_in": w(ks[4], (DEPTH, D_MODEL, IN_WIDTH), D_MODEL),
        "b_gate": 0.01 * jax.random.normal(ks[5], (DEPTH, N_BRANCH * D_MODEL), f32),
        "q_norm": gain(ks[6], (DEPTH, Q_LORA)),
        "w_uq": w(ks[7], (DEPTH, Q_LORA, MLA_HEADS * (MLA_NOPE + MLA_ROPE)), Q_LORA),
        "kv_norm": gain(ks[8], (DEPTH, KV_LORA)),
        "w_uk": w(ks[9], (DEPTH, KV_LORA, MLA_HEADS * MLA_NOPE), KV_LORA),
        "w_uv": w(ks[10], (DEPTH, KV_LORA, MLA_HEADS * MLA_V), KV_LORA),
        "mem_norm": gain(ks[11], (DEPTH, D_MODEL)),
        "w_mem_kv": w(ks[12], (DEPTH, D_MODEL, 2 * MEM_HEADS * MEM_DIM), D_MODEL),
        "w_branch_out": w(ks[13], (DEPTH, N_BRANCH, BRANCH_W, D_MODEL), BRANCH_W),
        "w_out": w(ks[14], (DEPTH, D_MODEL, D_MODEL), D_MODEL),
        "ln_mix_post": gain(ks[15], (DEPTH, D_MODEL)),
        "ln_mlp_pre": gain(ks[16], (DEPTH, D_MODEL)),
        "w_mlp_up": w(ks[17], (DEPTH, D_MODEL, D_FF), D_MODEL),
        "w_mlp_down": w(ks[18], (DEPTH, D_FF, D_MODEL), D_FF),
        "ln_mlp_post": gain(ks[19], (DEPTH, D_MODEL)),
    }


def reference(x, mem, positions, ln_mix_pre, w_in, b_gate, q_norm, w_uq, kv_norm, w_uk, w_uv,
              mem_norm, w_mem_kv, w_branch_out, w_out, ln_mix_post, ln_mlp_pre, w_mlp_up,
              w_mlp_down, ln_mlp_post):
    b, s, d = x.shape
    m = mem.shape[1]
    for l in range(DEPTH):
        h = rms_norm(x, ln_mix_pre[l])
        proj = jnp.einsum('bsd,de->bse', h, w_in[l])
        c_q, c_kv, k_r, sb_qkv, q_m, gate_logits = jnp.split(proj, IN_SPLITS, axis=-1)

        c_q = rms_norm(c_q, q_norm[l])
        q = jnp.einsum('bsr,re->bse', c_q, w_uq[l]).reshape(b, s, MLA_HEADS, MLA_NOPE + MLA_ROPE)
        q_nope, q_rope = q[..., :MLA_NOPE], apply_rope(q[..., MLA_NOPE:], positions)
        c_kv = rms_norm(c_kv, kv_norm[l])
        k_nope = jnp.einsum('bsr,re->bse', c_kv, w_uk[l]).reshape(b, s, MLA_HEADS, MLA_NOPE)
        v_mla = jnp.einsum('bsr,re->bse', c_kv, w_uv[l]).reshape(b, s, MLA_HEADS, MLA_V)
        k_rope = apply_rope(k_r[:, :, None, :], positions)
        q_full = jnp.concatenate([q_nope, q_rope], axis=-1)
        k_full = jnp.concatenate([k_nope, jnp.broadcast_to(k_rope, (b, s, MLA_HEADS, MLA_ROPE))], axis=-1)
        o_mla = causal_softmax_attention(q_full, k_full, v_mla, (MLA_NOPE + MLA_ROPE) ** -0.5)

        sq, sk, sv = jnp.split(sb_qkv.reshape(b, s, 3, SB_HEADS, SB_DIM), 3, axis=2)
        o_sb = stick_breaking_attention(sq[:, :, 0], sk[:, :, 0], sv[:, :, 0], SB_DIM ** -0.5)

        mem_h = rms_norm(mem, mem_norm[l])
        mkv = jnp.einsum('bmd,de->bme', mem_h, w_mem_kv[l]).reshape(b, m, 2, MEM_HEADS, MEM_DIM)
        o_mem = memory_cross_attention(q_m.reshape(b, s, MEM_HEADS, MEM_DIM), mkv[:, :, 0], mkv[:, :, 1],
                                       MEM_DIM ** -0.5)

        gates = jax.nn.sigmoid(gate_logits.astype(jnp.float32) + b_gate[l].astype(jnp.float32))
        gates = gates.reshape(b, s, N_BRANCH, d).astype(x.dtype)
        merged = None
        for i, o in enumerate((o_mla, o_sb, o_mem)):
            yb = gates[:, :, i] * jnp.einsum('bsc,cd->bsd', o, w_branch_out[l, i])
            merged = yb if merged is None else merged + yb
        y = jnp.einsum('bsd,de->bse', merged, w_out[l])
        x = x + rms_norm(y, ln_mix_post[l])

        h = rms_norm(x, ln_mlp_pre[l])
        u = jnp.square(jax.nn.relu(jnp.einsum('bsd,df->bsf', h, w_mlp_up[l])))
        x = x + rms_norm(jnp.einsum('bsf,fd->bsd', u, w_mlp_down[l]), ln_mlp_post[l])
    return x
```

```python
import math
from contextlib import ExitStack
import numpy as np
import concourse.bass as bass
import concourse.mybir as mybir
from concourse.bass_utils import run_bass_kernel_spmd

F32 = mybir.dt.float32
BF16 = mybir.dt.bfloat16
I32 = mybir.dt.int32
AF = mybir.ActivationFunctionType
ALU = mybir.AluOpType

NCORES = 8
NBC = 2
S = 2048
D = 1024
TG = 512
NG = S // TG
EPS = 1e-6
MASKV = -30000.0
ENGS = ("pe", "act", "dve", "pool", "sp")

C_GPRE, C_GMEM, C_GMLP, C_GQ, C_GKV, C_BG, C_INVF, C_PHASE, C_INVFLO = 0, 8, 16, 24, 27, 29, 53, 54, 55
NCOL = 56
STOP_AFTER = None
DEBUG_SCRATCH = False


class Res:
    __slots__ = ("name", "last_w", "readers", "excl")

    def __init__(self, name, excl=False):
        self.name = name
        self.last_w = None
        self.readers = []
        self.excl = excl


class Slot:
    def __init__(self, name):
        self.name = name
        self.count = 0
        self.sem = None


class Op:
    __slots__ = ("eng", "fn", "deps", "slot", "val", "needs_inc", "is_dma", "done")

    def __init__(self, eng, fn):
        self.eng = eng
        self.fn = fn
        self.deps = []
        self.slot = None
        self.val = None
        self.needs_inc = False
        self.is_dma = False
        self.done = False


class Prog:
    def __init__(self, nc, es):
        self.nc = nc
        self.es = es
        self.streams = {e: [] for e in ENGS}
        self.count = {e: 0 for e in ENGS}
        self.waited = {e: {} for e in ENGS}
        self.esem = {e: es.enter_context(nc.semaphore("sem_" + e)) for e in ("pe", "act", "dve", "pool")}
        self.slots = []
        self.n_ops = 0

    def slot(self, name):
        s = Slot(name)
        s.sem = self.es.enter_context(self.nc.semaphore("sl_" + name))
        self.slots.append(s)
        return s

    def _dep(self, o, d, raw):
        if d is None or d is o or d.done:
            return
        if (not d.is_dma) and (not o.is_dma) and d.eng == o.eng:
            if o.eng == "pe" or not raw:
                return
        o.deps.append(d)

    def op(self, eng, fn, reads=(), writes=(), slot=None):
        o = Op(eng, fn)
        if slot is not None:
            o.is_dma = True
            o.slot = slot
            slot.count += 1
            o.val = 16 * slot.count
        for r in reads:
            self._dep(o, r.last_w, True)
            if r.excl:
                for x in r.readers:
                    if x.eng != eng or x.is_dma:
                        self._dep(o, x, False)
        for w in writes:
            self._dep(o, w.last_w, True)
            for x in w.readers:
                self._dep(o, x, False)
        for d in o.deps:
            d.needs_inc = True
        for r in reads:
            r.readers.append(o)
        for w in writes:
            w.last_w = o
            w.readers = []
        self.streams[eng].append(o)
        self.n_ops += 1
        return o

    def dma(self, queue, out, in_, slot, reads=(), writes=()):
        if queue == "pool":
            return self.op(queue, lambda e: e.dma_start(out=out, in_=in_, max_dma_last_dim=4096), reads, writes, slot=slot)
        return self.op(queue, lambda e: e.dma_start(out=out, in_=in_), reads, writes, slot=slot)

    def _emit_stream(self, eng, e):
        waited = self.waited[eng]
        for o in self.streams[eng]:
            best = {}
            for d in o.deps:
                if d.is_dma:
                    key, sem, val = ("s", id(d.slot)), d.slot.sem, d.val
                else:
                    key, sem, val = ("e", d.eng), self.esem[d.eng], d.val
                if val > best.get(key, (None, 0))[1]:
                    best[key] = (sem, val)
            for key, (sem, val) in best.items():
                if waited.get(key, 0) >= val:
                    continue
                waited[key] = val
                e.wait_ge(sem, val)
            inst = o.fn(e)
            if o.is_dma:
                inst.then_inc(o.slot.sem, 16)
            elif o.needs_inc:
                inst.then_inc(self.esem[eng], 1)
            o.done = True
        for x in ("pe", "act", "dve", "pool"):
            if x != eng and waited.get(("e", x), 0) < self.count[x]:
                waited[("e", x)] = self.count[x]
                e.wait_ge(self.esem[x], self.count[x])
        for sl in self.slots:
            key = ("s", id(sl))
            if sl.count and waited.get(key, 0) < 16 * sl.count:
                waited[key] = 16 * sl.count
                e.wait_ge(sl.sem, 16 * sl.count)

    def flush(self):
        nc = self.nc
        for eng in ENGS:
            comp = [o for o in self.streams[eng] if not o.is_dma]
            if comp:
                comp[-1].needs_inc = True
            for o in self.streams[eng]:
                if (not o.is_dma) and o.needs_inc:
                    self.count[eng] += 1
                    o.val = self.count[eng]
        with nc.Block() as block:
            @block.tensor
            def _(e):
                self._emit_stream("pe", e)

            @block.scalar
            def _(e):
                self._emit_stream("act", e)

            @block.vector
            def _(e):
                self._emit_stream("dve", e)

            @block.gpsimd
            def _(e):
                self._emit_stream("pool", e)

            @block.sync
            def _(e):
                self._emit_stream("sp", e)
        self.streams = {e: [] for e in ENGS}


class Buf:
    def __init__(self, h, name):
        self.h = h
        self.name = name
        self.res = {}

    def r(self, key=None):
        if key not in self.res:
            self.res[key] = Res("%s:%s" % (self.name, key))
        return self.res[key]


def MM(P, out, lhsT, rhs, start, stop, rd, wr, skip=False):
    P.op("pe", lambda e: e.matmul(out, lhsT, rhs, start=start, stop=stop, skip_group_check=skip), rd, wr)


def TR(P, out, in_, ident, rd, wr):
    P.op("pe", lambda e: e.transpose(out=out, in_=in_, identity=ident), rd, wr)


def ACT(P, out, in_, func, rd, wr, **kw):
    P.op("act", lambda e: e.activation(out=out, in_=in_, func=func, **kw), rd, wr)


def TS(P, eng, out, in0, s1, op0, rd, wr, s2=None, op1=None):
    if op1 is None:
        P.op(eng, lambda e: e.tensor_scalar(out=out, in0=in0, scalar1=s1, scalar2=None, op0=op0), rd, wr)
    else:
        P.op(eng, lambda e: e.tensor_scalar(out=out, in0=in0, scalar1=s1, scalar2=s2, op0=op0, op1=op1), rd, wr)


def TT(P, eng, out, in0, in1, op, rd, wr):
    P.op(eng, lambda e: e.tensor_tensor(out=out, in0=in0, in1=in1, op=op), rd, wr)


def STT(P, out, in0, scalar, in1, op0, op1, rd, wr):
    P.op("dve", lambda e: e.scalar_tensor_tensor(out=out, in0=in0, scalar=scalar, in1=in1, op0=op0, op1=op1), rd, wr)


def CP(P, eng, out, in_, rd, wr):
    P.op(eng, lambda e: e.tensor_copy(out=out, in_=in_), rd, wr)


def RECIP(P, out, in_, rd, wr):
    P.op("dve", lambda e: e.reciprocal(out=out, in_=in_), rd, wr)


def MSET(P, eng, ap, val, wr):
    P.op(eng, lambda e: e.memset(ap, val), (), wr)


class Env:
    pass


class Rot:
    def __init__(self, items):
        self.items = items
        self.i = 0

    def get(self):
        it = self.items[self.i % len(self.items)]
        self.i += 1
        return it


class WStream:
    uid = 0

    def __init__(self, P, nc, es, name, shape, nbuf):
        self.P = P
        self.bufs = []
        for i in range(nbuf):
            WStream.uid += 1
            h = es.enter_context(nc.sbuf_tensor("ws%d_%s%d" % (WStream.uid, name, i), shape, BF16))
            self.bufs.append((h, Res("%s%d" % (name, i)), P.slot("%s%d_%d" % (name, i, WStream.uid))))
        self.i = 0

    def run(self, srcs, body, depth=None, view=None):
        n = len(srcs)
        depth = depth or (len(self.bufs) - 1)
        base = self.i
        self.i += n

        def issue(j):
            h, r, sl = self.bufs[(base + j) % len(self.bufs)]
            dst = view(h) if view else h[:]
            self.P.dma("pool", dst, srcs[j], sl, writes=[r])

        for j in range(min(depth, n)):
            issue(j)
        for j in range(n):
            h, r, sl = self.bufs[(base + j) % len(self.bufs)]
            body(j, h, r)
            if j + depth < n:
                issue(j + depth)


def norm_transpose_chunks(P, env, tiles, gcol0, dst, dst_res, stage, tb_groups):
    ntb = len(tiles)
    junk, ss, lnv, rstd, hn = stage["junk"], stage["ss"], stage["lnv"], stage["rstd"], stage["hn"]

    def stats():
        for grp in tb_groups:
            a, bnd = grp[0], grp[-1] + 1
            for tb in grp:
                xa, xr = tiles[tb]
                ACT(P, junk.h[:], xa, AF.Square, [xr], [junk.r(), ss.r(tb)], accum_out=ss.h[:, tb:tb + 1])
            ACT(P, lnv.h[:, a:bnd], ss.h[:, a:bnd], AF.Ln, [ss.r(t) for t in grp], [lnv.r(t) for t in grp],
                scale=1.0 / D, bias=EPS)
            ACT(P, rstd.h[:, a:bnd], lnv.h[:, a:bnd], AF.Exp, [lnv.r(t) for t in grp], [rstd.r(t) for t in grp],
                scale=-0.5)
            for tb in grp:
                xa, xr = tiles[tb]
                TS(P, "dve", hn.h[:, tb, :], xa, rstd.h[:, tb:tb + 1], ALU.mult, [xr, rstd.r(tb)], [hn.r(tb)])

    chunks = [stats]
    for kp in range(4):
        def tr(kp=kp):
            bank = env.pst[:, kp % 2, :]
            bres = env.tbank[kp % 2]
            for kk in range(2):
                k = 2 * kp + kk
                for tb in range(ntb):
                    TR(P, bank[:, kk * 512 + tb * 128: kk * 512 + (tb + 1) * 128], hn.h[:, tb, k * 128:(k + 1) * 128],
                       env.ident, [hn.r(tb), env.cres], [bres])
            for kk in range(2):
                k = 2 * kp + kk
                src = bank[:, kk * 512: kk * 512 + ntb * 128]
                g = env.cols[:, gcol0 + k: gcol0 + k + 1]
                if kk == 0:
                    TS(P, "dve", dst[:, k, 0:ntb * 128], src, g, ALU.mult, [bres, env.cres], [dst_res])
                else:
                    ACT(P, dst[:, k, 0:ntb * 128], src, AF.Copy, [bres, env.cres], [dst_res], scale=g)
        chunks.append(tr)
    return chunks


def norm_transpose(P, env, tiles, gcol0, dst, dst_res, stage, tb_groups):
    for ch in norm_transpose_chunks(P, env, tiles, gcol0, dst, dst_res, stage, tb_groups):
        ch()


def rstd_from_psum(P, env, bank_ap, bank_res, n_feat, tmp, out, cols):
    ACT(P, tmp.h[:, cols], bank_ap, AF.Ln, [bank_res], [tmp.r()], scale=1.0 / n_feat, bias=EPS)
    ACT(P, out.h[:, cols], tmp.h[:, cols], AF.Exp, [tmp.r()], [out.r()], scale=-0.5)


def build_program():
    nc = bass.Bass("TRN2", target_bir_lowering=False)

    def din(name, shape, dt=F32):
        return nc.dram_tensor(name, list(shape), dt, kind="ExternalInput").ap()

    x_d = din("x", [NBC, S, D])
    mem_d = din("mem", [NBC, 256, D])
    pos_d = din("pos", [NBC, S], I32)
    w1_d = din("w1", [18, 128, 8, 128])
    w1v_d = din("w1v", [128, 8, 512])
    wmk_d = din("wmk", [4, 128, 8, 128])
    wmv_d = din("wmv", [4, 128, 8, 128])
    wuq_d = din("wuq", [128, 3, 1024])
    wuk_d = din("wuk", [128, 2, 1024])
    wuv_d = din("wuv", [128, 2, 512])
    wg_d = din("wg", [24, 128, 8, 128])
    wb_d = din("wb", [8, 3, 128, 4, 128])
    wout_d = din("wout", [128, 8, 1024])
    wup_d = din("wup", [128, 8, 4096])
    wdn_d = din("wdn", [128, 32, 1024])
    cols_d = din("cols", [128, NCOL])
    gpost_d = din("gpost", [2, 1024])
    cmat_d = din("cmat", [128, 7, 128])
    out_d = nc.dram_tensor("out", [NBC, S, D], F32, kind="ExternalOutput").ap()
    O_d = nc.dram_tensor("oscr", [NBC, 12, 128, S], BF16,
                         **({"kind": "ExternalOutput"} if DEBUG_SCRATCH else {})).ap()

    with ExitStack() as es:
        P = Prog(nc, es)
        env = Env()

        uniq = [0]

        def sbuf(scope, name, shape, dt):
            uniq[0] += 1
            nm = "t%d_%s" % (uniq[0], name)
            return Buf(scope.enter_context(nc.sbuf_tensor(nm, list(shape), dt)), nm)

        cols_b = sbuf(es, "cols", [128, NCOL], F32)
        cmat_b = sbuf(es, "cmat", [128, 7, 128], BF16)
        env.cols = cols_b.h
        env.cres = Res("consts")
        s_const = P.slot("const")
        P.dma("sp", cols_b.h[:], cols_d, s_const, writes=[env.cres])
        s_const2 = P.slot("const2")
        P.dma("pool", cmat_b.h[:], cmat_d, s_const2, writes=[env.cres])
        env.ident = cmat_b.h[:, 0, :]
        env.ones = cmat_b.h[:, 1, :]
        env.tneg = cmat_b.h[:, 2, :]
        env.negones = cmat_b.h[:, 3, :]
        env.mneg_sb = cmat_b.h[:, 4, :]
        env.mneg_mla = cmat_b.h[:, 5, :]
        env.f2 = cmat_b.h[:, 6, :]

        ps = es.enter_context(nc.psum_tensor("ps", [128, 8, 512], F32))
        env.ps = ps
        env.pst = ps[:, 6:8, :].bitcast(BF16)
        env.bank = [Res("bank%d" % i, excl=True) for i in range(8)]
        env.tbank = [env.bank[6], env.bank[7]]

        with ExitStack() as sa:
            phase_A(P, nc, sa, env, x_d, mem_d, pos_d, w1_d, w1v_d, wmk_d, wmv_d, wuq_d, wuk_d, wuv_d, O_d, sbuf)
            P.flush()
        if STOP_AFTER != "A":
            with ExitStack() as sc:
                phase_C(P, nc, sc, env, x_d, O_d, wg_d, wb_d, wout_d, gpost_d, out_d, sbuf)
                P.flush()
            if STOP_AFTER != "C":
                with ExitStack() as sd:
                    phase_D(P, nc, sd, env, wup_d, wdn_d, gpost_d, out_d, sbuf)
                    P.flush()
    return nc


def phase_A(P, nc, sa, env, x_d, mem_d, pos_d, w1_d, w1v_d, wmk_d, wmv_d, wuq_d, wuk_d, wuv_d, O_d, sbuf):
    ps = env.ps
    cols = env.cols
    sbk = sbuf(sa, "sbk", [128, 4, S], BF16)
    sbv = sbuf(sa, "sbv", [128, 16, 512], BF16)
    kaug = sbuf(sa, "kaug", [128, 8, S], BF16)
    vmla = sbuf(sa, "vmla", [128, 16, 512], BF16)
    vaug = [sbuf(sa, "vaug%d" % i, [128, 16, 128], BF16) for i in range(2)]
    memK = sbuf(sa, "memK", [128, 4, 256], BF16)
    memV = sbuf(sa, "memV", [128, 2, 512], BF16)
    wuq = sbuf(sa, "wuq", [128, 3, 1024], BF16)
    wuk = sbuf(sa, "wuk", [128, 2, 1024], BF16)
    wuv = sbuf(sa, "wuv", [128, 2, 512], BF16)
    xs = [sbuf(sa, "xs%d" % i, [128, 1024], F32) for i in range(2)]
    xs_slot = [P.slot("xs%d" % i) for i in range(2)]
    stage = dict(junk=sbuf(sa, "junk", [128, 1024], BF16), ss=sbuf(sa, "ss", [128, 4], F32),
                 lnv=sbuf(sa, "lnv", [128, 4], F32), rstd=sbuf(sa, "rstd", [128, 4], F32),
                 hn=sbuf(sa, "hn", [128, 4, 1024], BF16))
    hT = sbuf(sa, "hT", [128, 8, TG], BF16)
    hT2 = sbuf(sa, "hT2", [128, 8, TG], BF16)
    w1v_slot = P.slot("w1v")
    wst = WStream(P, nc, sa, "wch", [128, 8, 128], 3)
    sq = [sbuf(sa, "sq%d" % i, [128, TG], BF16) for i in range(3)]
    cq = sbuf(sa, "cq", [128, 3, TG], BF16)
    ckv = sbuf(sa, "ckv", [128, 2, TG], BF16)
    yk = sbuf(sa, "yk", [128, TG], BF16)
    cs = sbuf(sa, "cs", [128, TG], F32)
    cs2 = sbuf(sa, "cs2", [128, TG], F32)
    pos_slot = P.slot("pos")
    sbq = sbuf(sa, "sbq", [128, 4, TG], BF16)
    qm = sbuf(sa, "qm", [128, 4, TG], BF16)
    qaug = sbuf(sa, "qaug", [128, 8, TG], BF16)
    omla = sbuf(sa, "omla", [128, 4, TG], BF16)
    ost_slot = [P.slot("ost%d" % i) for i in range(3)]
    ef = [sbuf(sa, "ef%d" % i, [128, 2, TG], F32) for i in range(2)]
    lsum = [sbuf(sa, "lsum%d" % i, [128, 2, TG], BF16) for i in range(2)]
    rec = [sbuf(sa, "rec%d" % i, [128, TG], F32) for i in range(2)]
    scrA = sbuf(sa, "scrA", [128, 5, 2, TG], BF16)

    class View:
        def __init__(self, h, res):
            self.h = h
            self._r = res

        def r(self, key=None):
            return self._r
    aT = [View(scrA.h[:, i], scrA.r(i)) for i in range(3)]
    lp = [View(scrA.h[:, 3 + i], scrA.r(3 + i)) for i in range(2)]
    w1vb_h = scrA.h[:, 0:4].rearrange("p a b t -> p (a b) t")
    alias_set = [scrA.r(i) for i in range(5)]
    ang_h, kf_h = ef[0].h[:, 0, :], ef[0].h[:, 1, :]
    posi_b = sbuf(sa, "posi", [128, TG], I32)
    posi_i = posi_b.h[:]
    ki_i = posi_b.h[:]
    ang = View(ang_h, ef[0].r())
    kf = View(kf_h, ef[0].r())
    posi = posi_b
    ki = posi_b

    lntmp, rq = rec[0], rec[1]

    class rkv:
        h = ef[1].h[:, 1, :]

        @staticmethod
        def r(key=None):
            return ef[1].r()
    s_w = P.slot("smallw")
    s_w2 = P.slot("smallw2")
    s_w3 = P.slot("smallw3")

    def load_small_weights():
        P.dma("pool", wuq.h[:], wuq_d, s_w, writes=[wuq.r()])
        P.dma("pool", wuk.h[:], wuk_d, s_w2, writes=[wuk.r()])
        P.dma("pool", wuv.h[:], wuv_d, s_w3, writes=[wuv.r()])
        for i in range(2):
            MSET(P, "pool", vaug[i].h[:, :, 64:128], 1.0, [vaug[i].r("ones")])

    banks = Rot([(ps[:, i, :], env.bank[i]) for i in range(6)])

    def load_x_tile(src_ap, i):
        P.dma("sp", xs[i].h[:], src_ap, xs_slot[i], writes=[xs[i].r()])

    pro_banks = Rot([(ps[:, 6 + i, :], env.bank[6 + i]) for i in range(2)])

    def project_chunks(srcs, consume, ncols, hT=hT, banks=banks):
        def body(j, h, r):
            bk, br = banks.get()
            for k in range(8):
                MM(P, bk[:, 0:ncols], h[:, k, :], hT.h[:, k, 0:ncols], k == 0, k == 7, [r, hT.r()], [br])
            consume(j, bk[:, 0:ncols], br)
        wst.run(srcs, body)

    def project_v(src_d, ntb, dst, tb0, hT=hT, banks=banks):
        P.dma("pool", w1vb_h, src_d, w1v_slot, writes=alias_set)
        for tb in range(ntb):
            bk, br = banks.get()
            for k in range(8):
                MM(P, bk, hT.h[:, k, tb * 128:(tb + 1) * 128], w1vb_h[:, k, :], k == 0, k == 7, [hT.r()] + alias_set, [br])
            if tb % 2 == 0:
                CP(P, "dve", dst.h[:, tb0 + tb, :], bk, [br], [dst.r(tb0 + tb)])
            else:
                ACT(P, dst.h[:, tb0 + tb, :], bk, AF.Copy, [br], [dst.r(tb0 + tb)])

    def mem_consume(j, bk, br):
        CP(P, "dve", memK.h[:, j, :], bk, [br], [memK.r()])

    def mem_chunks(b):
        def loads():
            for tb in range(2):
                load_x_tile(mem_d[b, tb * 128:(tb + 1) * 128, :], tb)
        nt = norm_transpose_chunks(P, env, [(xs[0].h[:], xs[0].r()), (xs[1].h[:], xs[1].r())], C_GMEM, hT.h, hT.r(),
                                   stage, [[0, 1]])
        def proj_v():
            bks = [pro_banks.get() for _ in range(2)]

            def body(c, h, r):
                for tb in range(2):
                    bk, br = bks[tb]
                    for k in range(8):
                        MM(P, bk[:, c * 128:(c + 1) * 128], hT.h[:, k, tb * 128:(tb + 1) * 128], h[:, k, :], k == 0, k == 7,
                           [hT.r(), r], [br], skip=True)
            wst.run([wmv_d[c] for c in range(4)], body)
            for tb in range(2):
                bk, br = bks[tb]
                CP(P, "dve", memV.h[:, tb, :], bk, [br], [memV.r(tb)])
        return [loads] + nt + [lambda: project_chunks([wmk_d[j] for j in range(4)], mem_consume, 256, hT=hT, banks=pro_banks),
                               proj_v]

    def rope_chunks(b, g, cs):
        t0 = g * TG

        def c0():
            P.dma("sp", posi_i, pos_d[b:b + 1, t0:t0 + TG].partition_broadcast(128), pos_slot, writes=[posi.r()])

        def c1():
            CP(P, "dve", kf.h, posi_i, [posi.r()], [kf.r()])
            TS(P, "dve", ang.h, kf.h, cols[:, C_INVF:C_INVF + 1], ALU.mult, [kf.r(), env.cres], [ang.r()],
               s2=cols[:, C_PHASE:C_PHASE + 1], op1=ALU.add)
            STT(P, ang.h, kf.h, cols[:, C_INVFLO:C_INVFLO + 1], ang.h, ALU.mult, ALU.add, [kf.r(), ang.r(), env.cres], [ang.r()])
            TS(P, "dve", kf.h, ang.h, 1.0 / (2 * math.pi), ALU.mult, [ang.r()], [kf.r()])
            CP(P, "dve", ki_i, kf.h, [kf.r()], [ki.r()])

        def c2():
            CP(P, "dve", kf.h, ki_i, [ki.r()], [kf.r()])
            C1 = 6.28125
            C2 = 2 * math.pi - C1
            STT(P, ang.h, kf.h, -C1, ang.h, ALU.mult, ALU.add, [kf.r(), ang.r()], [ang.r()])
            STT(P, ang.h, kf.h, -C2, ang.h, ALU.mult, ALU.add, [kf.r(), ang.r()], [ang.r()])
            TS(P, "dve", ang.h, ang.h, 3.1415925, ALU.min, [ang.r()], [ang.r()], s2=-3.1415925, op1=ALU.max)

        def c3():
            ACT(P, cs.h[:], ang.h, AF.Sin, [ang.r()], [cs.r()])
        return [c0, c1, c2, c3]

    def front_preload(b, g):
        for i in range(2):
            load_x_tile(x_d[b, g * TG + i * 128: g * TG + (i + 1) * 128, :], i)

    def front_chunks(b, g, hT):
        t0 = g * TG
        junk, ss, lnv, rstd, hn = stage["junk"], stage["ss"], stage["lnv"], stage["rstd"], stage["hn"]
        chunks = []
        for half in range(2):
            grp = [half * 2, half * 2 + 1]

            def stats(half=half, grp=grp):
                for i, tb in enumerate(grp):
                    ACT(P, junk.h[:], xs[i].h[:], AF.Square, [xs[i].r()], [junk.r(), ss.r(tb)], accum_out=ss.h[:, tb:tb + 1])
                a, bnd = grp[0], grp[-1] + 1
                ACT(P, lnv.h[:, a:bnd], ss.h[:, a:bnd], AF.Ln, [ss.r(t) for t in grp], [lnv.r(t) for t in grp],
                    scale=1.0 / D, bias=EPS)
                ACT(P, rstd.h[:, a:bnd], lnv.h[:, a:bnd], AF.Exp, [lnv.r(t) for t in grp], [rstd.r(t) for t in grp],
                    scale=-0.5)

            def scale_(half=half, grp=grp):
                for i, tb in enumerate(grp):
                    TS(P, "dve", hn.h[:, tb, :], xs[i].h[:], rstd.h[:, tb:tb + 1], ALU.mult, [xs[i].r(), rstd.r(tb)], [hn.r(tb)])
                if half == 0:
                    for i in range(2):
                        load_x_tile(x_d[b, t0 + (2 + i) * 128: t0 + (3 + i) * 128, :], i)
            chunks += [stats, scale_]
        for kp in range(4):
            def tr(kp=kp):
                bank = env.pst[:, kp % 2, :]
                bres = env.tbank[kp % 2]
                for kk in range(2):
                    k = 2 * kp + kk
                    for tb in range(4):
                        TR(P, bank[:, kk * 512 + tb * 128: kk * 512 + (tb + 1) * 128], hn.h[:, tb, k * 128:(k + 1) * 128],
                           env.ident, [hn.r(tb), env.cres], [bres])

            def ev(kp=kp):
                bank = env.pst[:, kp % 2, :]
                bres = env.tbank[kp % 2]
                for kk in range(2):
                    k = 2 * kp + kk
                    src = bank[:, kk * 512:(kk + 1) * 512]
                    gsc = cols[:, C_GPRE + k: C_GPRE + k + 1]
                    if kk == 0:
                        TS(P, "dve", hT.h[:, k, :], src, gsc, ALU.mult, [bres, env.cres], [hT.r()])
                    else:
                        ACT(P, hT.h[:, k, :], src, AF.Copy, [bres, env.cres], [hT.r()], scale=gsc)
            chunks += [tr, ev]
        return chunks

    hT_bufs = [hT, hT2]
    cs_bufs = [cs, cs2]

    def prologue_chunks(b):
        ch = [load_small_weights] if b == 0 else []
        ch += mem_chunks(b)
        ch += [lambda: front_preload(b, 0)]
        ch += rope_chunks(b, 0, cs_bufs[0]) + front_chunks(b, 0, hT_bufs[0])
        return ch

    for ch in prologue_chunks(0):
        ch()
    for b, g in [(b_, g_) for b_ in range(NBC) for g_ in range(NG)]:
        t0 = g * TG
        hT = hT_bufs[g % 2]
        cs = cs_bufs[g % 2]

        def consume(c, bk, br):
            if c < 3:
                CP(P, "dve", cq.h[:, c, :], bk, [br], [cq.r(c)])
                ACT(P, sq[c % 3].h[:], bk, AF.Square, [br], [sq[c % 3].r()])
                if c == 2:
                    sb_, sr_ = banks.get()
                    for j in range(3):
                        MM(P, sb_, env.ones, sq[j].h[:], j == 0, j == 2, [sq[j].r(), env.cres], [sr_])
                    rstd_from_psum(P, env, sb_, sr_, 384, lntmp, rq, slice(0, TG))
                    for j in range(3):
                        STT(P, cq.h[:, j, :], cq.h[:, j, :], cols[:, C_GQ + j:C_GQ + j + 1], rq.h[:], ALU.mult, ALU.mult,
                            [cq.r(j), rq.r(), env.cres], [cq.r(j)])
            elif c < 5:
                j = c - 3
                CP(P, "dve", ckv.h[:, j, :], bk, [br], [ckv.r(j)])
                ACT(P, sq[j].h[:], bk, AF.Square, [br], [sq[j].r()])
                if j == 1:
                    sb_, sr_ = banks.get()
                    for jj in range(2):
                        MM(P, sb_, env.ones, sq[jj].h[:], jj == 0, jj == 1, [sq[jj].r(), env.cres], [sr_])
                    rstd_from_psum(P, env, sb_, sr_, 256, lntmp, rkv, slice(0, TG))
                    for jj in range(2):
                        STT(P, ckv.h[:, jj, :], ckv.h[:, jj, :], cols[:, C_GKV + jj:C_GKV + jj + 1], rkv.h[:], ALU.mult,
                            ALU.mult, [ckv.r(jj), rkv.r(), env.cres], [ckv.r(jj)])
            elif c == 5:
                TT(P, "dve", yk.h[:], bk, cs.h[:], ALU.mult, [br, cs.r()], [yk.r()])
            elif c < 10:
                j = c - 6
                ACT(P, sbq.h[:, j, :], bk, AF.Copy, [br], [sbq.r((j, 0)), sbq.r((j, 1))], scale=0.125)
            elif c < 14:
                j = c - 10
                CP(P, "dve", sbk.h[:, j, t0:t0 + TG], bk, [br], [sbk.r((j, g))])
            else:
                j = c - 14
                ACT(P, qm.h[:, j, :], bk, AF.Copy, [br], [qm.r(j)])
        rope_next = rope_chunks(b, g + 1, cs_bufs[(g + 1) % 2]) if g + 1 < NG else []
        if rope_next:
            rope_next[0]()
        project_chunks([w1_d[c] for c in range(18)], consume, TG, hT=hT)
        for ch in rope_next[1:]:
            ch()
        project_v(w1v_d, 4, sbv, g * 4, hT=hT)

        for h in range(8):
            bk, br = banks.get()
            for k in range(3):
                MM(P, bk, wuq.h[:, k, h * 128:(h + 1) * 128], cq.h[:, k, :], k == 0, k == 2, [wuq.r(), cq.r(k)], [br])
            TT(P, "dve", qaug.h[:, h, :], bk, cs.h[:], ALU.mult, [br, cs.r()], [qaug.r(h)])
        for h in range(8):
            bk, br = banks.get()
            for k in range(2):
                MM(P, bk, wuk.h[:, k, h * 128:(h + 1) * 128], ckv.h[:, k, :], k == 0, False, [wuk.r(), ckv.r(k)], [br])
            MM(P, bk, env.f2, yk.h[:], False, True, [yk.r(), env.cres], [br])
            ACT(P, kaug.h[:, h, t0:t0 + TG], bk, AF.Copy, [br], [kaug.r((h, g))])
        for tb in range(4):
            bk, br = banks.get()
            for k in range(2):
                MM(P, bk, ckv.h[:, k, tb * 128:(tb + 1) * 128], wuv.h[:, k, :], k == 0, k == 1, [ckv.r(k), wuv.r()], [br])
            CP(P, "dve", vmla.h[:, g * 4 + tb, :], bk, [br], [vmla.r(g * 4 + tb)])

        nkb = 4 * g + 4
        abanks = Rot([(ps[:, i, :], env.bank[i]) for i in range(8)])
        pbanks = Rot([(ps[:, 2 * i:2 * i + 2, :], [env.bank[2 * i], env.bank[2 * i + 1]]) for i in range(3)])
        obanks = Rot([(ps[:, 6 + i, :], env.bank[6 + i]) for i in range(2)])
        mla_s = Rot([(ps[:, i, :], env.bank[i]) for i in range(4)])
        mla_o = Rot([(ps[:, 4 + i, :], env.bank[4 + i]) for i in range(2)])

        MSCALE = 128 ** -0.5
        mtasks = []
        for h in range(4):
            mtasks.append(mem_task(P, env, h, MSCALE, memK, memV, qm, aT[h % 2], rec[h % 2], ef[h % 2],
                                   [abanks.get() for _ in range(4)]))
        run_pipeline(mtasks, [("Z", 1), ("PV", 0), ("N", -1)])
        P.dma("sp", O_d[b, 8:12, :, t0:t0 + TG].rearrange("c p t -> p c t"), qm.h[:], ost_slot[2],
              reads=[qm.r(j) for j in range(4)])

        if g + 1 < NG:
            front_preload(b, g + 1)
        tasks = []
        ti = 0
        for hp in range(4):
            ob, obr = obanks.get()
            ls = lsum[hp % 2]
            for idx, kb in enumerate(range(nkb - 1, -1, -1)):
                tasks.append(sb_task(P, env, g, hp, idx, kb, nkb, ti, ob, obr, ls, sbq, sbk, sbv, pbanks.get(), ef, lp, aT))
                ti += 1
        run_pipeline(tasks, [("Z", 1), ("A", -1), ("L", 0), ("E", 1), ("CUMA", 0), ("AV", -1), ("CUMB", 0)])
        P.dma("sp", O_d[b, 4:8, :, t0:t0 + TG].rearrange("c p t -> p c t"), sbq.h[:], ost_slot[1],
              reads=[sbq.r((j, i)) for j in range(4) for i in range(2)])

        extras = []
        if g + 1 < NG:
            extras = front_chunks(b, g + 1, hT_bufs[(g + 1) % 2])
        elif b + 1 < NBC:
            extras = prologue_chunks(b + 1)

        tasks = []
        ti = 0
        for h in range(8):
            ob, obr = mla_o.get()
            for kb in range(nkb):
                tasks.append(mla_task(P, env, g, h, kb, nkb, ti, ob, obr, kaug, qaug, vmla, vaug[h % 2], omla, rec[h % 2],
                                      mla_s.get(), aT, lt=ef[1] if (g <= 1 and h % 2 == 1) else None))
                ti += 1
        run_pipeline(tasks, [("Z", 1), ("P", 0), ("PV", -1)], extras=extras,
                     every=max(1, len(tasks) // (len(extras) + 1)))
        P.dma("sp", O_d[b, 0:4, :, t0:t0 + TG].rearrange("c p t -> p c t"), omla.h[:], ost_slot[0],
              reads=[omla.r((j, i)) for j in range(4) for i in range(2)])


def mem_task(P, env, h, scale, memK, memV, qm, a2, rc, lt, bks):
    (zb0, zr0), (zb1, zr1), (ob, obr), (db, dbr) = bks
    zs = [(zb0, zr0), (zb1, zr1)]

    def fZ():
        for mt in range(2):
            MM(P, zs[mt][0], memK.h[:, h, mt * 128:(mt + 1) * 128], qm.h[:, h, :], True, True, [memK.r(), qm.r(h)], [zs[mt][1]])
        for mt in range(2):
            ACT(P, a2.h[:, mt, :], zs[mt][0], AF.Exp, [zs[mt][1]], [a2.r()], scale=scale)

    def fPV():
        for mt in range(2):
            MM(P, ob, memV.h[:, mt, h * 128:(h + 1) * 128], a2.h[:, mt, :], mt == 0, mt == 1, [memV.r(mt), a2.r()], [obr])
        for mt in range(2):
            MM(P, db, env.ones, a2.h[:, mt, :], mt == 0, mt == 1, [a2.r(), env.cres], [dbr])

    def fN():
        ACT(P, lt.h[:, 0, :], db, AF.Ln, [dbr], [lt.r()])
        ACT(P, rc.h[:], lt.h[:, 0, :], AF.Exp, [lt.r()], [rc.r()], scale=-1.0)
        TT(P, "dve", qm.h[:, h, :], ob, rc.h[:], ALU.mult, [obr, rc.r()], [qm.r(h)])

    return {"Z": fZ, "PV": fPV, "N": fN}


def run_pipeline(tasks, order, extras=(), every=3):
    n = len(tasks)
    lo = min(off for _, off in order)
    hi = max(off for _, off in order)
    extras = list(extras)
    for cnt, s_ in enumerate(range(-hi, n - lo)):
        for name, off in order:
            i = s_ + off
            if 0 <= i < n:
                tasks[i][name]()
        if extras and cnt % every == every - 1:
            extras.pop(0)()
    for ex in extras:
        ex()


def sb_task(P, env, g, hp, idx, kb, nkb, ti, ob, obr, ls, sbq, sbk, sbv, zbank, ef, lp, aT):
    j = hp
    jd = kb - 4 * g
    c0 = 128 * jd if jd >= 0 else 0
    kres = sbk.r((j, kb // 4))
    zb2, zr = zbank
    e_, l_, a = ef[ti % 2], lp[ti % 2], aT[ti % 3]

    def fZ():
        for i in range(2):
            po = 64 * i
            MM(P, zb2[:, i, c0:TG], sbk.h[po:po + 64, j, kb * 128:(kb + 1) * 128], sbq.h[po:po + 64, j, c0:TG], True, jd < 0,
               [kres, sbq.r((j, i))], [zr[i]])
        if jd >= 0:
            for i in range(2):
                MM(P, zb2[:, i, c0:c0 + 128], env.ident, env.mneg_sb, False, True, [env.cres], [zr[i]])

    def fE():
        ACT(P, e_.h[:, :, c0:TG], zb2[:, :, c0:TG], AF.Exp, zr, [e_.r()])

    def fL():
        ACT(P, l_.h[:, :, c0:TG], e_.h[:, :, c0:TG], AF.Ln, [e_.r()], [l_.r()], bias=1.0)

    def fCUMA():
        if idx == 0:
            MSET(P, "pool", ls.h[:], 0.0, [ls.r()])
        if idx > 0:
            c1 = 128 * (jd + 1) if jd >= 0 else 0
            for i in range(2):
                MM(P, zb2[:, i, c1:TG], env.negones, ls.h[:, i, c1:TG], False, True, [ls.r(), env.cres], [zr[i]], skip=True)

    def fCUMB():
        for i in range(2):
            MM(P, zb2[:, i, c0:TG], env.tneg, l_.h[:, i, c0:TG], False, True, [l_.r(), env.cres], [zr[i]], skip=True)
        if kb > 0:
            TT(P, "dve", ls.h[:, :, c0:TG], ls.h[:, :, c0:TG], l_.h[:, :, c0:TG], ALU.add, [ls.r(), l_.r()], [ls.r()])

    def fA():
        ACT(P, a.h[:, :, c0:TG], zb2[:, :, c0:TG], AF.Exp, zr, [a.r()])

    def fAV():
        for i in range(2):
            po = 64 * i
            h = 2 * hp + i
            MM(P, ob[po:po + 64, c0:TG], sbv.h[:, kb, h * 64:(h + 1) * 64], a.h[:, i, c0:TG], idx == 0, kb == 0,
               [sbv.r(kb), a.r()], [obr], skip=True)
        if kb == 0:
            CP(P, "dve", sbq.h[:, j, :], ob, [obr], [sbq.r((j, 0)), sbq.r((j, 1))])

    return {"Z": fZ, "E": fE, "L": fL, "CUMA": fCUMA, "CUMB": fCUMB, "A": fA, "AV": fAV}


def mla_task(P, env, g, h, kb, nkb, ti, ob, obr, kaug, qaug, vmla, va, omla, rc, zbank, aT, lt=None):
    ASCALE = 96 ** -0.5
    j, po = h // 2, (h % 2) * 64
    jd = kb - 4 * g
    c0 = 128 * jd if jd >= 0 else 0
    zb, zr = zbank
    a3 = aT[ti % 3]
    a_res = a3.r()

    class a:
        h = a3.h[:, 0, :]

        @staticmethod
        def r():
            return a_res

    def fZ():
        if kb == 0:
            CP(P, "pool", va.h[:, 0:nkb, 0:64], vmla.h[:, 0:nkb, h * 64:(h + 1) * 64],
               [vmla.r(t) for t in range(nkb)], [va.r("v")])
        MM(P, zb[:, c0:TG], kaug.h[:, h, kb * 128:(kb + 1) * 128], qaug.h[:, h, c0:TG], True, jd < 0,
           [kaug.r((h, kb // 4)), qaug.r(h)], [zr])
        if jd >= 0:
            MM(P, zb[:, c0:c0 + 128], env.ident, env.mneg_mla, False, True, [env.cres], [zr])

    def fP():
        ACT(P, a.h[:, c0:TG], zb[:, c0:TG], AF.Exp, [zr], [a.r()], scale=ASCALE)

    def fPV():
        MM(P, ob[:, c0:TG], va.h[:, kb, :], a.h[:, c0:TG], kb == 0, kb == nkb - 1,
           [va.r("v"), va.r("ones"), a.r()], [obr], skip=True)
        if kb == nkb - 1:
            if lt is not None:
                ACT(P, lt.h[0:64, 0, :], ob[64:128, :], AF.Ln, [obr], [lt.r()])
                ACT(P, rc.h[0:64, :], lt.h[0:64, 0, :], AF.Exp, [lt.r()], [rc.r()], scale=-1.0)
            else:
                RECIP(P, rc.h[0:64, :], ob[64:128, :], [obr], [rc.r()])
            TT(P, "dve", omla.h[po:po + 64, j, :], ob[0:64, :], rc.h[0:64, :], ALU.mult, [obr, rc.r()],
               [omla.r((j, h % 2))])

    return {"Z": fZ, "P": fP, "PV": fPV}


def phase_C(P, nc, sc, env, x_d, O_d, wg_d, wb_d, wout_d, gpost_d, out_d, sbuf):
    ps = env.ps
    cols = env.cols
    wg = sbuf(sc, "wg", [128, 24, 8, 128], BF16)
    wb = sbuf(sc, "wbr", [128, 24, 4, 128], BF16)
    wout = sbuf(sc, "wout", [128, 8, 1024], BF16)
    gbc = sbuf(sc, "gbc", [128, 1024], F32)
    og = [sbuf(sc, "og%d" % i, [128, 12, TG], BF16) for i in range(2)]
    og_slot = [P.slot("og%d" % i) for i in range(2)]
    xr = [sbuf(sc, "xr%d" % i, [128, 4, 1024], F32) for i in range(2)]
    xr_slot = [P.slot("xr%d" % i) for i in range(2)]
    st_slot = [P.slot("stc%d" % i) for i in range(2)]
    stage = dict(junk=sbuf(sc, "junkc", [128, 1024], BF16), ss=sbuf(sc, "ssc", [128, 4], F32),
                 lnv=sbuf(sc, "lnvc", [128, 4], F32), rstd=sbuf(sc, "rstdc", [128, 4], F32),
                 hn=sbuf(sc, "hnc", [128, 4, 1024], BF16))
    hT = sbuf(sc, "hTc", [128, 8, TG], BF16)
    merged = sbuf(sc, "merged", [128, 8, TG], BF16)
    gt = [[sbuf(sc, "gt%d_%d" % (r, i), [128, TG], F32) for i in range(3)] for r in range(2)]
    ssy = sbuf(sc, "ssy", [128, 4], F32)
    lny = sbuf(sc, "lny", [128, 4], F32)
    rsy = sbuf(sc, "rsy", [128, 4], F32)
    tt = [sbuf(sc, "tt%d" % i, [128, 1024], F32) for i in range(2)]

    wslots = [P.slot("wc%d" % i) for i in range(17)]
    def load_weights(first_dep):
        for e in range(8):
            P.dma("pool", wg.h[:, e * 3:(e + 1) * 3, :, :], wg_d[e * 3:(e + 1) * 3].rearrange("c p k m -> p c k m"),
                  wslots[e], reads=first_dep if e == 0 else (), writes=[wg.r(e)])
            P.dma("pool", wb.h[:, e * 3:(e + 1) * 3, :, :], wb_d[e].rearrange("i p k m -> p i k m"), wslots[8 + e],
                  writes=[wb.r(e)])
            if e == 1:
                P.dma("pool", wout.h[:], wout_d, wslots[16], writes=[wout.r()])
    s_g = P.slot("gbc")

    banks = Rot([(ps[:, i, :], env.bank[i]) for i in range(6)])
    ybanks = Rot([(ps[:, 2 * i:2 * i + 2, :], (env.bank[2 * i], env.bank[2 * i + 1])) for i in range(3)])
    groups = [(b, g) for b in range(NBC) for g in range(NG)]

    def issue_loads(n):
        b, g = groups[n]
        t0 = g * TG
        P.dma("sp", og[n % 2].h[:], O_d[b, :, :, t0:t0 + TG].rearrange("c p t -> p c t"), og_slot[n % 2],
              writes=[og[n % 2].r()])
        P.dma("sp", xr[n % 2].h[:], x_d[b, t0:t0 + TG, :].rearrange("(t p) d -> p t d", p=128), xr_slot[n % 2],
              writes=[xr[n % 2].r(t) for t in range(4)])

    def prep(n):
        xb_ = xr[n % 2]
        return norm_transpose_chunks(P, env, [(xb_.h[:, t, :], xb_.r(t)) for t in range(4)], C_GPRE, hT.h, hT.r(),
                                     stage, [[0, 1, 2, 3]])

    issue_loads(0)
    P.dma("sp", gbc.h[:], gpost_d[0:1, :].partition_broadcast(128), s_g, writes=[gbc.r()])
    load_weights([xr[0].r(t) for t in range(4)])
    for n, (b, g) in enumerate(groups):
        t0 = g * TG
        if n + 1 < len(groups):
            issue_loads(n + 1)
        xb = xr[n % 2]
        ogb = og[n % 2]
        if n == 0:
            for ch in prep(0):
                ch()
        pchunks = prep(n + 1) if n + 1 < len(groups) else []
        for e in range(8):
            if e == 3 and pchunks:
                pchunks[0]()
            gts = gt[e % 2]
            for i in range(3):
                bk, br = banks.get()
                c = i * 8 + e
                for k in range(8):
                    MM(P, bk, wg.h[:, e * 3 + i, k, :], hT.h[:, k, :], k == 0, k == 7, [wg.r(e), hT.r()], [br])
                ACT(P, gts[i].h[:], bk, AF.Sigmoid, [br, env.cres], [gts[i].r()], bias=cols[:, C_BG + c:C_BG + c + 1])
                bk2, br2 = banks.get()
                for k in range(4):
                    MM(P, bk2, wb.h[:, e * 3 + i, k, :], ogb.h[:, i * 4 + k, :], k == 0, k == 3, [wb.r(e), ogb.r()], [br2])
                TT(P, "dve", gts[i].h[:], bk2, gts[i].h[:], ALU.mult, [br2, gts[i].r()], [gts[i].r()])
            TT(P, "pool", gts[0].h[:], gts[0].h[:], gts[1].h[:], ALU.add, [gts[0].r(), gts[1].r()], [gts[0].r()])
            TT(P, "pool", merged.h[:, e, :], gts[0].h[:], gts[2].h[:], ALU.add, [gts[0].r(), gts[2].r()], [merged.r()])
        for ch in pchunks[1:]:
            ch()
        for tb in range(4):
            yb, (yr0, yr1) = ybanks.get()
            for half in range(2):
                for k in range(8):
                    MM(P, yb[:, half, :], merged.h[:, k, tb * 128:(tb + 1) * 128], wout.h[:, k, half * 512:(half + 1) * 512],
                       k == 0, k == 7, [merged.r(), wout.r()], [yr0 if half == 0 else yr1])
            ACT(P, stage["junk"].h[:].rearrange("p (a b) -> p a b", a=2), yb, AF.Square, [yr0, yr1], [stage["junk"].r(), ssy.r(tb)],
                accum_out=ssy.h[:, tb:tb + 1])
            ACT(P, lny.h[:, tb:tb + 1], ssy.h[:, tb:tb + 1], AF.Ln, [ssy.r(tb)], [lny.r(tb)], scale=1.0 / D, bias=EPS)
            ACT(P, rsy.h[:, tb:tb + 1], lny.h[:, tb:tb + 1], AF.Exp, [lny.r(tb)], [rsy.r(tb)], scale=-0.5)
            t_ = tt[tb % 2]
            for half in range(2):
                STT(P, t_.h[:, half * 512:(half + 1) * 512], yb[:, half, :], rsy.h[:, tb:tb + 1],
                    gbc.h[:, half * 512:(half + 1) * 512], ALU.mult, ALU.mult,
                    [yr0 if half == 0 else yr1, rsy.r(tb), gbc.r()], [t_.r()])
            TT(P, "pool", xb.h[:, tb, :], xb.h[:, tb, :], t_.h[:], ALU.add, [xb.r(tb), t_.r()], [xb.r(tb)])
        P.dma("sp", out_d[b, t0:t0 + TG, :].rearrange("(t p) d -> p t d", p=128), xb.h[:], st_slot[n % 2],
              reads=[xb.r(t) for t in range(4)])


def phase_D(P, nc, sd, env, wup_d, wdn_d, gpost_d, out_d, sbuf):
    ps = env.ps
    wup = sbuf(sd, "wup", [128, 8, 4096], BF16)
    wdn = sbuf(sd, "wdn", [128, 32, 1024], BF16)
    gbc = sbuf(sd, "gbd", [128, 1024], F32)
    UT = 256
    x1 = [sbuf(sd, "x1_%d" % i, [128, 2, 1024], F32) for i in range(2)]
    x1_slot = [P.slot("x1_%d" % i) for i in range(2)]
    st_slot = [P.slot("std%d" % i) for i in range(2)]
    stage = dict(junk=sbuf(sd, "junkd", [128, 1024], BF16), ss=sbuf(sd, "ssd", [128, 4], F32),
                 lnv=sbuf(sd, "lnvd", [128, 4], F32), rstd=sbuf(sd, "rstdd", [128, 4], F32),
                 hn=sbuf(sd, "hnd", [128, 2, 1024], BF16))
    h2T = [sbuf(sd, "h2T%d" % i, [128, 8, UT], BF16) for i in range(2)]
    rl = [sbuf(sd, "rl%d" % i, [128, UT], BF16) for i in range(3)]
    uT = [sbuf(sd, "uT%d" % i, [128, UT], BF16) for i in range(8)]
    ssz = sbuf(sd, "ssz", [128, 4], F32)
    ssz2 = sbuf(sd, "ssz2", [128, 2], F32)
    lnz = sbuf(sd, "lnz", [128, 2], F32)
    rsz = sbuf(sd, "rsz", [128, 2], F32)
    tt = [sbuf(sd, "ttd%d" % i, [128, 1024], F32) for i in range(2)]

    wslots = [P.slot("wd%d" % i) for i in range(16)]
    def load_weights(first_dep):
        for q in range(8):
            P.dma("pool", wup.h[:, :, q * 512:(q + 1) * 512], wup_d[:, :, q * 512:(q + 1) * 512], wslots[q],
                  reads=first_dep if q == 0 else (), writes=[wup.r(q)])
            P.dma("pool", wdn.h[:, q * 4:(q + 1) * 4, :], wdn_d[:, q * 4:(q + 1) * 4, :], wslots[8 + q], writes=[wdn.r(q)])
    s_g = P.slot("gbd")

    zbank = [(ps[:, i, :], env.bank[i]) for i in range(4)]
    ubanks = Rot([(ps[:, 4 + i, :], env.bank[4 + i]) for i in range(2)])
    units = [(b, u) for b in range(NBC) for u in range(S // UT)]

    def issue_load(n):
        b, u = units[n]
        t0 = u * UT
        P.dma("sp", x1[n % 2].h[:], out_d[b, t0:t0 + UT, :].rearrange("(t p) d -> p t d", p=128), x1_slot[n % 2],
              writes=[x1[n % 2].r(t) for t in range(2)])

    def prep(n):
        xb_ = x1[n % 2]
        return norm_transpose_chunks(P, env, [(xb_.h[:, t, :], xb_.r(t)) for t in range(2)], C_GMLP, h2T[n % 2].h,
                                     h2T[n % 2].r(), stage, [[0, 1]])

    DEPTH = 3
    NU = DEPTH + 2
    issue_load(0)
    P.dma("sp", gbc.h[:], gpost_d[1:2, :].partition_broadcast(128), s_g, writes=[gbc.r()])
    load_weights([x1[0].r(t) for t in range(2)])
    for ch in prep(0):
        ch()
    for n, (b, u) in enumerate(units):
        t0 = u * UT
        if n + 1 < len(units):
            issue_load(n + 1)
        xb = x1[n % 2]
        hcur = h2T[n % 2]

        def up(f):
            ub, ur = ubanks.get()
            for k in range(8):
                MM(P, ub[:, 0:UT], wup.h[:, k, f * 128:(f + 1) * 128], hcur.h[:, k, :], k == 0, k == 7,
                   [wup.r(f // 4), hcur.r()], [ur])
            r_ = rl[f % 3]
            u_ = uT[f % NU]
            ACT(P, r_.h[:], ub[:, 0:UT], AF.Relu, [ur], [r_.r()])
            TT(P, "dve", u_.h[:], r_.h[:], r_.h[:], ALU.mult, [r_.r()], [u_.r()])

        def down(f):
            u_ = uT[f % NU]
            for tb in range(2):
                for half in range(2):
                    zb, zr = zbank[tb * 2 + half]
                    MM(P, zb, u_.h[:, tb * 128:(tb + 1) * 128], wdn.h[:, f, half * 512:(half + 1) * 512], f == 0, f == 31,
                       [u_.r(), wdn.r(f // 4)], [zr])

        pchunks = prep(n + 1) if n + 1 < len(units) else []
        sched = {8: 0, 18: 1, 21: 2, 24: 3, 27: 4}
        for f in range(32 + DEPTH):
            if f < 32:
                up(f)
            if f in sched and pchunks:
                pchunks[sched[f]]()
            if f - DEPTH >= 0:
                down(f - DEPTH)
        for tb in range(2):
            for half in range(2):
                zb, zr = zbank[tb * 2 + half]
                ACT(P, stage["junk"].h[:, 0:512], zb, AF.Square, [zr], [stage["junk"].r(), ssz.r(tb * 2 + half)],
                    accum_out=ssz.h[:, tb * 2 + half: tb * 2 + half + 1])
            TT(P, "dve", ssz2.h[:, tb:tb + 1], ssz.h[:, 2 * tb:2 * tb + 1], ssz.h[:, 2 * tb + 1:2 * tb + 2], ALU.add,
               [ssz.r(2 * tb), ssz.r(2 * tb + 1)], [ssz2.r(tb)])
            ACT(P, lnz.h[:, tb:tb + 1], ssz2.h[:, tb:tb + 1], AF.Ln, [ssz2.r(tb)], [lnz.r(tb)], scale=1.0 / D, bias=EPS)
            ACT(P, rsz.h[:, tb:tb + 1], lnz.h[:, tb:tb + 1], AF.Exp, [lnz.r(tb)], [rsz.r(tb)], scale=-0.5)
            t_ = tt[tb % 2]
            for half in range(2):
                zb, zr = zbank[tb * 2 + half]
                STT(P, t_.h[:, half * 512:(half + 1) * 512], zb, rsz.h[:, tb:tb + 1], gbc.h[:, half * 512:(half + 1) * 512],
                    ALU.mult, ALU.mult, [zr, rsz.r(tb), gbc.r()], [t_.r()])
            TT(P, "pool", xb.h[:, tb, :], xb.h[:, tb, :], t_.h[:], ALU.add, [xb.r(tb), t_.r()], [xb.r(tb)])
        P.dma("sp", out_d[b, t0:t0 + UT, :].rearrange("(t p) d -> p t d", p=128), xb.h[:], st_slot[n % 2],
              reads=[xb.r(t) for t in range(2)])


def _chunkify(W):
    C = W.shape[1]
    return np.ascontiguousarray(W.reshape(8, 128, C // 128, 128).transpose(2, 1, 0, 3))


def _rows(W, nk):
    return np.ascontiguousarray(W.reshape(nk, 128, W.shape[1]).transpose(1, 0, 2))


def prep_shared(inp):
    f = np.float32
    w_in = np.asarray(inp["w_in"], f)[0]
    kr = np.zeros((D, 128), f)
    kr[:, 64:96] = w_in[:, 640:672]
    kr[:, 96:112] = w_in[:, 656:672]
    kr[:, 112:128] = w_in[:, 640:656]
    w1cat = np.concatenate([w_in[:, 0:640], kr, w_in[:, 672:1184], w_in[:, 1184:1696], w_in[:, 2208:2720]], axis=1)
    sh = {}
    sh["w1"] = _chunkify(w1cat)
    sh["w1v"] = _rows(w_in[:, 1696:2208], 8)
    wmkv = np.asarray(inp["w_mem_kv"], f)[0]
    sh["wmk"] = _chunkify(wmkv[:, 0:512])
    sh["wmv"] = _chunkify(wmkv[:, 512:1024])
    w_uq = np.asarray(inp["w_uq"], f)[0]
    wq = np.zeros((384, 1024), f)
    for h in range(8):
        s = h * 96
        wq[:, h * 128:h * 128 + 64] = w_uq[:, s:s + 64]
        wq[:, h * 128 + 64:h * 128 + 96] = w_uq[:, s + 64:s + 96]
        wq[:, h * 128 + 96:h * 128 + 112] = w_uq[:, s + 80:s + 96]
        wq[:, h * 128 + 112:h * 128 + 128] = w_uq[:, s + 64:s + 80]
    sh["wuq"] = _rows(wq, 3)
    w_uk = np.asarray(inp["w_uk"], f)[0]
    wk = np.zeros((256, 1024), f)
    for h in range(8):
        wk[:, h * 128:h * 128 + 64] = w_uk[:, h * 64:(h + 1) * 64]
    sh["wuk"] = _rows(wk, 2)
    sh["wuv"] = _rows(np.asarray(inp["w_uv"], f)[0], 2)
    wgc = _chunkify(w_in[:, 2720:5792])
    sh["wg"] = np.ascontiguousarray(wgc.reshape(3, 8, 128, 8, 128).transpose(1, 0, 2, 3, 4).reshape(24, 128, 8, 128))
    wbo = np.asarray(inp["w_branch_out"], f)[0]
    sh["wb"] = np.ascontiguousarray(wbo.reshape(3, 4, 128, 8, 128).transpose(3, 0, 2, 1, 4))
    sh["wout"] = _rows(np.asarray(inp["w_out"], f)[0], 8)
    sh["wup"] = _rows(np.asarray(inp["w_mlp_up"], f)[0], 8)
    sh["wdn"] = _rows(np.asarray(inp["w_mlp_down"], f)[0], 32)
    cols = np.zeros((128, NCOL), f)
    cols[:, C_GPRE:C_GPRE + 8] = np.asarray(inp["ln_mix_pre"], f)[0].reshape(8, 128).T
    cols[:, C_GMEM:C_GMEM + 8] = np.asarray(inp["mem_norm"], f)[0].reshape(8, 128).T
    cols[:, C_GMLP:C_GMLP + 8] = np.asarray(inp["ln_mlp_pre"], f)[0].reshape(8, 128).T
    cols[:, C_GQ:C_GQ + 3] = np.asarray(inp["q_norm"], f)[0].reshape(3, 128).T
    cols[:, C_GKV:C_GKV + 2] = np.asarray(inp["kv_norm"], f)[0].reshape(2, 128).T
    cols[:, C_BG:C_BG + 24] = np.asarray(inp["b_gate"], f)[0].reshape(24, 128).T
    invf64 = 1.0 / (10000.0 ** (np.arange(16, dtype=np.float64) * (2.0 / 32)))
    invf = invf64.astype(f)
    invf_lo = (invf64 - invf.astype(np.float64)).astype(f)
    ivl = np.zeros(128, f)
    ivl[64:80] = invf_lo
    ivl[80:96] = invf_lo
    ivl[96:112] = -invf_lo
    ivl[112:128] = invf_lo
    cols[:, C_INVFLO] = ivl
    iv = np.zeros(128, f)
    ph = np.zeros(128, f)
    ph[0:96] = np.pi / 2
    iv[64:80] = invf
    iv[80:96] = invf
    iv[96:112] = -invf
    iv[112:128] = invf
    cols[:, C_INVF] = iv
    cols[:, C_PHASE] = ph
    sh["cols"] = cols
    sh["gpost"] = np.stack([np.asarray(inp["ln_mix_post"], f)[0], np.asarray(inp["ln_mlp_post"], f)[0]])
    jj = np.arange(128)[:, None]
    tt = np.arange(128)[None, :]
    cm = np.zeros((7, 128, 128), f)
    cm[0] = np.eye(128, dtype=f)
    cm[1] = 1.0
    cm[2] = np.where(jj >= tt, -1.0, 0.0)
    cm[3] = -1.0
    cm[4] = np.where(tt <= jj, MASKV, 0.0)
    cm[5] = np.where(tt < jj, MASKV, 0.0)
    for j in range(64):
        cm[6][64 + j, 64 + j % 32] = 1.0
        cm[6][64 + j, 96 + j % 32] = 1.0
    sh["cmat"] = np.ascontiguousarray(cm.transpose(1, 0, 2))
    return sh


_NC_CACHE = {}


def kernel(**inputs):
    x = np.ascontiguousarray(np.asarray(inputs["x"], np.float32))
    mem = np.ascontiguousarray(np.asarray(inputs["mem"], np.float32))
    pos = np.ascontiguousarray(np.asarray(inputs["positions"], np.int32))
    sh = prep_shared(inputs)
    if "nc" not in _NC_CACHE:
        _NC_CACHE["nc"] = build_program()
    nc = _NC_CACHE["nc"]
    in_maps = []
    for c in range(NCORES):
        m = dict(sh)
        m["x"] = x[c * NBC:(c + 1) * NBC]
        m["mem"] = mem[c * NBC:(c + 1) * NBC]
        m["pos"] = pos[c * NBC:(c + 1) * NBC]
        in_maps.append(m)
    res = run_bass_kernel_spmd(nc, in_maps, core_ids=list(range(NCORES)))
    kernel.last_results = res
    return np.concatenate([np.asarray(r["out"], np.float32) for r in res.results], axis=0)
```

```python
import math
from contextlib import ExitStack
import numpy as np
import concourse.bass as bass
import concourse.mybir as mybir
from concourse.bass_utils import run_bass_kernel_spmd

F32 = mybir.dt.float32
BF16 = mybir.dt.bfloat16
I32 = mybir.dt.int32
AF = mybir.ActivationFunctionType
ALU = mybir.AluOpType

NCORES = 8
NBC = 2
S = 2048
D = 1024
TG = 512
NG = S // TG
EPS = 1e-6
MASKV = -30000.0
ENGS = ("pe", "act", "dve", "pool", "sp")

C_GPRE, C_GMEM, C_GMLP, C_GQ, C_GKV, C_BG, C_INVF, C_PHASE, C_INVFLO = 0, 8, 16, 24, 27, 29, 53, 54, 55
NCOL = 56
STOP_AFTER = None
DEBUG_SCRATCH = False


class Res:
    __slots__ = ("name", "last_w", "readers", "excl")

    def __init__(self, name, excl=False):
        self.name = name
        self.last_w = None
        self.readers = []
        self.excl = excl


class Slot:
    def __init__(self, name):
        self.name = name
        self.count = 0
        self.sem = None


class Op:
    __slots__ = ("eng", "fn", "deps", "slot", "val", "needs_inc", "is_dma", "done")

    def __init__(self, eng, fn):
        self.eng = eng
        self.fn = fn
        self.deps = []
        self.slot = None
        self.val = None
        self.needs_inc = False
        self.is_dma = False
        self.done = False


class Prog:
    def __init__(self, nc, es):
        self.nc = nc
        self.es = es
        self.streams = {e: [] for e in ENGS}
        self.count = {e: 0 for e in ENGS}
        self.waited = {e: {} for e in ENGS}
        self.esem = {e: es.enter_context(nc.semaphore("sem_" + e)) for e in ("pe", "act", "dve", "pool")}
        self.slots = []
        self.n_ops = 0

    def slot(self, name):
        s = Slot(name)
        s.sem = self.es.enter_context(self.nc.semaphore("sl_" + name))
        self.slots.append(s)
        return s

    def _dep(self, o, d, raw):
        if d is None or d is o or d.done:
            return
        if (not d.is_dma) and (not o.is_dma) and d.eng == o.eng:
            if o.eng == "pe" or not raw:
                return
        o.deps.append(d)

    def op(self, eng, fn, reads=(), writes=(), slot=None):
        o = Op(eng, fn)
        if slot is not None:
            o.is_dma = True
            o.slot = slot
            slot.count += 1
            o.val = 16 * slot.count
        for r in reads:
            self._dep(o, r.last_w, True)
            if r.excl:
                for x in r.readers:
                    if x.eng != eng or x.is_dma:
                        self._dep(o, x, False)
        for w in writes:
            self._dep(o, w.last_w, True)
            for x in w.readers:
                self._dep(o, x, False)
        for d in o.deps:
            d.needs_inc = True
        for r in reads:
            r.readers.append(o)
        for w in writes:
            w.last_w = o
            w.readers = []
        self.streams[eng].append(o)
        self.n_ops += 1
        return o

    def dma(self, queue, out, in_, slot, reads=(), writes=()):
        if queue == "pool":
            return self.op(queue, lambda e: e.dma_start(out=out, in_=in_, max_dma_last_dim=4096), reads, writes, slot=slot)
        return self.op(queue, lambda e: e.dma_start(out=out, in_=in_), reads, writes, slot=slot)

    def _emit_stream(self, eng, e):
        waited = self.waited[eng]
        for o in self.streams[eng]:
            best = {}
            for d in o.deps:
                if d.is_dma:
                    key, sem, val = ("s", id(d.slot)), d.slot.sem, d.val
                else:
                    key, sem, val = ("e", d.eng), self.esem[d.eng], d.val
                if val > best.get(key, (None, 0))[1]:
                    best[key] = (sem, val)
            for key, (sem, val) in best.items():
                if waited.get(key, 0) >= val:
                    continue
                waited[key] = val
                e.wait_ge(sem, val)
            inst = o.fn(e)
            if o.is_dma:
                inst.then_inc(o.slot.sem, 16)
            elif o.needs_inc:
                inst.then_inc(self.esem[eng], 1)
            o.done = True
        for x in ("pe", "act", "dve", "pool"):
            if x != eng and waited.get(("e", x), 0) < self.count[x]:
                waited[("e", x)] = self.count[x]
                e.wait_ge(self.esem[x], self.count[x])
        for sl in self.slots:
            key = ("s", id(sl))
            if sl.count and waited.get(key, 0) < 16 * sl.count:
                waited[key] = 16 * sl.count
                e.wait_ge(sl.sem, 16 * sl.count)

    def flush(self):
        nc = self.nc
        for eng in ENGS:
            comp = [o for o in self.streams[eng] if not o.is_dma]
            if comp:
                comp[-1].needs_inc = True
            for o in self.streams[eng]:
                if (not o.is_dma) and o.needs_inc:
                    self.count[eng] += 1
                    o.val = self.count[eng]
        with nc.Block() as block:
            @block.tensor
            def _(e):
                self._emit_stream("pe", e)

            @block.scalar
            def _(e):
                self._emit_stream("act", e)

            @block.vector
            def _(e):
                self._emit_stream("dve", e)

            @block.gpsimd
            def _(e):
                self._emit_stream("pool", e)

            @block.sync
            def _(e):
                self._emit_stream("sp", e)
        self.streams = {e: [] for e in ENGS}


class Buf:
    def __init__(self, h, name):
        self.h = h
        self.name = name
        self.res = {}

    def r(self, key=None):
        if key not in self.res:
            self.res[key] = Res("%s:%s" % (self.name, key))
        return self.res[key]


def MM(P, out, lhsT, rhs, start, stop, rd, wr, skip=False):
    P.op("pe", lambda e: e.matmul(out, lhsT, rhs, start=start, stop=stop, skip_group_check=skip), rd, wr)


def TR(P, out, in_, ident, rd, wr):
    P.op("pe", lambda e: e.transpose(out=out, in_=in_, identity=ident), rd, wr)


def ACT(P, out, in_, func, rd, wr, **kw):
    P.op("act", lambda e: e.activation(out=out, in_=in_, func=func, **kw), rd, wr)


def TS(P, eng, out, in0, s1, op0, rd, wr, s2=None, op1=None):
    if op1 is None:
        P.op(eng, lambda e: e.tensor_scalar(out=out, in0=in0, scalar1=s1, scalar2=None, op0=op0), rd, wr)
    else:
        P.op(eng, lambda e: e.tensor_scalar(out=out, in0=in0, scalar1=s1, scalar2=s2, op0=op0, op1=op1), rd, wr)


def TT(P, eng, out, in0, in1, op, rd, wr):
    P.op(eng, lambda e: e.tensor_tensor(out=out, in0=in0, in1=in1, op=op), rd, wr)


def STT(P, out, in0, scalar, in1, op0, op1, rd, wr):
    P.op("dve", lambda e: e.scalar_tensor_tensor(out=out, in0=in0, scalar=scalar, in1=in1, op0=op0, op1=op1), rd, wr)


def CP(P, eng, out, in_, rd, wr):
    P.op(eng, lambda e: e.tensor_copy(out=out, in_=in_), rd, wr)


def RECIP(P, out, in_, rd, wr):
    P.op("dve", lambda e: e.reciprocal(out=out, in_=in_), rd, wr)


def MSET(P, eng, ap, val, wr):
    P.op(eng, lambda e: e.memset(ap, val), (), wr)


class Env:
    pass


class Rot:
    def __init__(self, items):
        self.items = items
        self.i = 0

    def get(self):
        it = self.items[self.i % len(self.items)]
        self.i += 1
        return it


class WStream:
    uid = 0

    def __init__(self, P, nc, es, name, shape, nbuf):
        self.P = P
        self.bufs = []
        for i in range(nbuf):
            WStream.uid += 1
            h = es.enter_context(nc.sbuf_tensor("ws%d_%s%d" % (WStream.uid, name, i), shape, BF16))
            self.bufs.append((h, Res("%s%d" % (name, i)), P.slot("%s%d_%d" % (name, i, WStream.uid))))
        self.i = 0

    def run(self, srcs, body, depth=None, view=None):
        n = len(srcs)
        depth = depth or (len(self.bufs) - 1)
        base = self.i
        self.i += n

        def issue(j):
            h, r, sl = self.bufs[(base + j) % len(self.bufs)]
            dst = view(h) if view else h[:]
            self.P.dma("pool", dst, srcs[j], sl, writes=[r])

        for j in range(min(depth, n)):
            issue(j)
        for j in range(n):
            h, r, sl = self.bufs[(base + j) % len(self.bufs)]
            body(j, h, r)
            if j + depth < n:
                issue(j + depth)


def norm_transpose_chunks(P, env, tiles, gcol0, dst, dst_res, stage, tb_groups):
    ntb = len(tiles)
    junk, ss, lnv, rstd, hn = stage["junk"], stage["ss"], stage["lnv"], stage["rstd"], stage["hn"]

    def stats():
        for grp in tb_groups:
            a, bnd = grp[0], grp[-1] + 1
            for tb in grp:
                xa, xr = tiles[tb]
                ACT(P, junk.h[:], xa, AF.Square, [xr], [junk.r(), ss.r(tb)], accum_out=ss.h[:, tb:tb + 1])
            ACT(P, lnv.h[:, a:bnd], ss.h[:, a:bnd], AF.Ln, [ss.r(t) for t in grp], [lnv.r(t) for t in grp],
                scale=1.0 / D, bias=EPS)
            ACT(P, rstd.h[:, a:bnd], lnv.h[:, a:bnd], AF.Exp, [lnv.r(t) for t in grp], [rstd.r(t) for t in grp],
                scale=-0.5)
            for tb in grp:
                xa, xr = tiles[tb]
                TS(P, "dve", hn.h[:, tb, :], xa, rstd.h[:, tb:tb + 1], ALU.mult, [xr, rstd.r(tb)], [hn.r(tb)])

    chunks = [stats]
    for kp in range(4):
        def tr(kp=kp):
            bank = env.pst[:, kp % 2, :]
            bres = env.tbank[kp % 2]
            for kk in range(2):
                k = 2 * kp + kk
                for tb in range(ntb):
                    TR(P, bank[:, kk * 512 + tb * 128: kk * 512 + (tb + 1) * 128], hn.h[:, tb, k * 128:(k + 1) * 128],
                       env.ident, [hn.r(tb), env.cres], [bres])
            for kk in range(2):
                k = 2 * kp + kk
                src = bank[:, kk * 512: kk * 512 + ntb * 128]
                g = env.cols[:, gcol0 + k: gcol0 + k + 1]
                if kk == 0:
                    TS(P, "dve", dst[:, k, 0:ntb * 128], src, g, ALU.mult, [bres, env.cres], [dst_res])
                else:
                    ACT(P, dst[:, k, 0:ntb * 128], src, AF.Copy, [bres, env.cres], [dst_res], scale=g)
        chunks.append(tr)
    return chunks


def norm_transpose(P, env, tiles, gcol0, dst, dst_res, stage, tb_groups):
    for ch in norm_transpose_chunks(P, env, tiles, gcol0, dst, dst_res, stage, tb_groups):
        ch()


def rstd_from_psum(P, env, bank_ap, bank_res, n_feat, tmp, out, cols):
    ACT(P, tmp.h[:, cols], bank_ap, AF.Ln, [bank_res], [tmp.r()], scale=1.0 / n_feat, bias=EPS)
    ACT(P, out.h[:, cols], tmp.h[:, cols], AF.Exp, [tmp.r()], [out.r()], scale=-0.5)


def build_program():
    nc = bass.Bass("TRN2", target_bir_lowering=False)

    def din(name, shape, dt=F32):
        return nc.dram_tensor(name, list(shape), dt, kind="ExternalInput").ap()

    x_d = din("x", [NBC, S, D])
    mem_d = din("mem", [NBC, 256, D])
    pos_d = din("pos", [NBC, S], I32)
    w1_d = din("w1", [18, 128, 8, 128])
    w1v_d = din("w1v", [128, 8, 512])
    wmk_d = din("wmk", [4, 128, 8, 128])
    wmv_d = din("wmv", [4, 128, 8, 128])
    wuq_d = din("wuq", [128, 3, 1024])
    wuk_d = din("wuk", [128, 2, 1024])
    wuv_d = din("wuv", [128, 2, 512])
    wg_d = din("wg", [24, 128, 8, 128])
    wb_d = din("wb", [8, 3, 128, 4, 128])
    wout_d = din("wout", [128, 8, 1024])
    wup_d = din("wup", [128, 8, 4096])
    wdn_d = din("wdn", [128, 32, 1024])
    cols_d = din("cols", [128, NCOL])
    gpost_d = din("gpost", [2, 1024])
    cmat_d = din("cmat", [128, 7, 128])
    out_d = nc.dram_tensor("out", [NBC, S, D], F32, kind="ExternalOutput").ap()
    O_d = nc.dram_tensor("oscr", [NBC, 12, 128, S], BF16,
                         **({"kind": "ExternalOutput"} if DEBUG_SCRATCH else {})).ap()

    with ExitStack() as es:
        P = Prog(nc, es)
        env = Env()

        uniq = [0]

        def sbuf(scope, name, shape, dt):
            uniq[0] += 1
            nm = "t%d_%s" % (uniq[0], name)
            return Buf(scope.enter_context(nc.sbuf_tensor(nm, list(shape), dt)), nm)

        cols_b = sbuf(es, "cols", [128, NCOL], F32)
        cmat_b = sbuf(es, "cmat", [128, 7, 128], BF16)
        env.cols = cols_b.h
        env.cres = Res("consts")
        s_const = P.slot("const")
        P.dma("sp", cols_b.h[:], cols_d, s_const, writes=[env.cres])
        s_const2 = P.slot("const2")
        P.dma("pool", cmat_b.h[:], cmat_d, s_const2, writes=[env.cres])
        env.ident = cmat_b.h[:, 0, :]
        env.ones = cmat_b.h[:, 1, :]
        env.tneg = cmat_b.h[:, 2, :]
        env.negones = cmat_b.h[:, 3, :]
        env.mneg_sb = cmat_b.h[:, 4, :]
        env.mneg_mla = cmat_b.h[:, 5, :]
        env.f2 = cmat_b.h[:, 6, :]

        ps = es.enter_context(nc.psum_tensor("ps", [128, 8, 512], F32))
        env.ps = ps
        env.pst = ps[:, 6:8, :].bitcast(BF16)
        env.bank = [Res("bank%d" % i, excl=True) for i in range(8)]
        env.tbank = [env.bank[6], env.bank[7]]

        with ExitStack() as sa:
            phase_A(P, nc, sa, env, x_d, mem_d, pos_d, w1_d, w1v_d, wmk_d, wmv_d, wuq_d, wuk_d, wuv_d, O_d, sbuf)
            P.flush()
        if STOP_AFTER != "A":
            with ExitStack() as sc:
                phase_C(P, nc, sc, env, x_d, O_d, wg_d, wb_d, wout_d, gpost_d, out_d, sbuf)
                P.flush()
            if STOP_AFTER != "C":
                with ExitStack() as sd:
                    phase_D(P, nc, sd, env, wup_d, wdn_d, gpost_d, out_d, sbuf)
                    P.flush()
    return nc


def phase_A(P, nc, sa, env, x_d, mem_d, pos_d, w1_d, w1v_d, wmk_d, wmv_d, wuq_d, wuk_d, wuv_d, O_d, sbuf):
    ps = env.ps
    cols = env.cols
    sbk = sbuf(sa, "sbk", [128, 4, S], BF16)
    sbv = sbuf(sa, "sbv", [128, 16, 512], BF16)
    kaug = sbuf(sa, "kaug", [128, 8, S], BF16)
    vmla = sbuf(sa, "vmla", [128, 16, 512], BF16)
    vaug = [sbuf(sa, "vaug%d" % i, [128, 16, 128], BF16) for i in range(2)]
    memK = sbuf(sa, "memK", [128, 4, 256], BF16)
    memV = sbuf(sa, "memV", [128, 2, 512], BF16)
    wuq = sbuf(sa, "wuq", [128, 3, 1024], BF16)
    wuk = sbuf(sa, "wuk", [128, 2, 1024], BF16)
    wuv = sbuf(sa, "wuv", [128, 2, 512], BF16)
    xs = [sbuf(sa, "xs%d" % i, [128, 1024], F32) for i in range(2)]
    xs_slot = [P.slot("xs%d" % i) for i in range(2)]
    stage = dict(junk=sbuf(sa, "junk", [128, 1024], BF16), ss=sbuf(sa, "ss", [128, 4], F32),
                 lnv=sbuf(sa, "lnv", [128, 4], F32), rstd=sbuf(sa, "rstd", [128, 4], F32),
                 hn=sbuf(sa, "hn", [128, 4, 1024], BF16))
    hT = sbuf(sa, "hT", [128, 8, TG], BF16)
    hT2 = sbuf(sa, "hT2", [128, 8, TG], BF16)
    w1v_slot = P.slot("w1v")
    wst = WStream(P, nc, sa, "wch", [128, 8, 128], 3)
    sq = [sbuf(sa, "sq%d" % i, [128, TG], BF16) for i in range(3)]
    cq = sbuf(sa, "cq", [128, 3, TG], BF16)
    ckv = sbuf(sa, "ckv", [128, 2, TG], BF16)
    yk = sbuf(sa, "yk", [128, TG], BF16)
    cs = sbuf(sa, "cs", [128, TG], F32)
    cs2 = sbuf(sa, "cs2", [128, TG], F32)
    pos_slot = P.slot("pos")
    sbq = sbuf(sa, "sbq", [128, 4, TG], BF16)
    qm = sbuf(sa, "qm", [128, 4, TG], BF16)
    qaug = sbuf(sa, "qaug", [128, 8, TG], BF16)
    omla = sbuf(sa, "omla", [128, 4, TG], BF16)
    ost_slot = [P.slot("ost%d" % i) for i in range(3)]
    ef = [sbuf(sa, "ef%d" % i, [128, 2, TG], F32) for i in range(2)]
    lsum = [sbuf(sa, "lsum%d" % i, [128, 2, TG], BF16) for i in range(2)]
    rec = [sbuf(sa, "rec%d" % i, [128, TG], F32) for i in range(2)]
    scrA = sbuf(sa, "scrA", [128, 5, 2, TG], BF16)

    class View:
        def __init__(self, h, res):
            self.h = h
            self._r = res

        def r(self, key=None):
            return self._r
    aT = [View(scrA.h[:, i], scrA.r(i)) for i in range(3)]
    lp = [View(scrA.h[:, 3 + i], scrA.r(3 + i)) for i in range(2)]
    w1vb_h = scrA.h[:, 0:4].rearrange("p a b t -> p (a b) t")
    alias_set = [scrA.r(i) for i in range(5)]
    ang_h, kf_h = ef[0].h[:, 0, :], ef[0].h[:, 1, :]
    posi_b = sbuf(sa, "posi", [128, TG], I32)
    posi_i = posi_b.h[:]
    ki_i = posi_b.h[:]
    ang = View(ang_h, ef[0].r())
    kf = View(kf_h, ef[0].r())
    posi = posi_b
    ki = posi_b

    lntmp, rq = rec[0], rec[1]

    class rkv:
        h = ef[1].h[:, 1, :]

        @staticmethod
        def r(key=None):
            return ef[1].r()
    s_w = P.slot("smallw")
    s_w2 = P.slot("smallw2")
    s_w3 = P.slot("smallw3")

    def load_small_weights():
        P.dma("pool", wuq.h[:], wuq_d, s_w, writes=[wuq.r()])
        P.dma("pool", wuk.h[:], wuk_d, s_w2, writes=[wuk.r()])
        P.dma("pool", wuv.h[:], wuv_d, s_w3, writes=[wuv.r()])
        for i in range(2):
            MSET(P, "pool", vaug[i].h[:, :, 64:128], 1.0, [vaug[i].r("ones")])

    banks = Rot([(ps[:, i, :], env.bank[i]) for i in range(6)])

    def load_x_tile(src_ap, i):
        P.dma("sp", xs[i].h[:], src_ap, xs_slot[i], writes=[xs[i].r()])

    pro_banks = Rot([(ps[:, 6 + i, :], env.bank[6 + i]) for i in range(2)])

    def project_chunks(srcs, consume, ncols, hT=hT, banks=banks):
        def body(j, h, r):
            bk, br = banks.get()
            for k in range(8):
                MM(P, bk[:, 0:ncols], h[:, k, :], hT.h[:, k, 0:ncols], k == 0, k == 7, [r, hT.r()], [br])
            consume(j, bk[:, 0:ncols], br)
        wst.run(srcs, body)

    def project_v(src_d, ntb, dst, tb0, hT=hT, banks=banks):
        P.dma("pool", w1vb_h, src_d, w1v_slot, writes=alias_set)
        for tb in range(ntb):
            bk, br = banks.get()
            for k in range(8):
                MM(P, bk, hT.h[:, k, tb * 128:(tb + 1) * 128], w1vb_h[:, k, :], k == 0, k == 7, [hT.r()] + alias_set, [br])
            if tb % 2 == 0:
                CP(P, "dve", dst.h[:, tb0 + tb, :], bk, [br], [dst.r(tb0 + tb)])
            else:
                ACT(P, dst.h[:, tb0 + tb, :], bk, AF.Copy, [br], [dst.r(tb0 + tb)])

    def mem_consume(j, bk, br):
        CP(P, "dve", memK.h[:, j, :], bk, [br], [memK.r()])

    def mem_chunks(b):
        def loads():
            for tb in range(2):
                load_x_tile(mem_d[b, tb * 128:(tb + 1) * 128, :], tb)
        nt = norm_transpose_chunks(P, env, [(xs[0].h[:], xs[0].r()), (xs[1].h[:], xs[1].r())], C_GMEM, hT.h, hT.r(),
                                   stage, [[0, 1]])
        def proj_v():
            bks = [pro_banks.get() for _ in range(2)]

            def body(c, h, r):
                for tb in range(2):
                    bk, br = bks[tb]
                    for k in range(8):
                        MM(P, bk[:, c * 128:(c + 1) * 128], hT.h[:, k, tb * 128:(tb + 1) * 128], h[:, k, :], k == 0, k == 7,
                           [hT.r(), r], [br], skip=True)
            wst.run([wmv_d[c] for c in range(4)], body)
            for tb in range(2):
                bk, br = bks[tb]
                CP(P, "dve", memV.h[:, tb, :], bk, [br], [memV.r(tb)])
        return [loads] + nt + [lambda: project_chunks([wmk_d[j] for j in range(4)], mem_consume, 256, hT=hT, banks=pro_banks),
                               proj_v]

    def rope_chunks(b, g, cs):
        t0 = g * TG

        def c0():
            P.dma("sp", posi_i, pos_d[b:b + 1, t0:t0 + TG].partition_broadcast(128), pos_slot, writes=[posi.r()])

        def c1():
            CP(P, "dve", kf.h, posi_i, [posi.r()], [kf.r()])
            TS(P, "dve", ang.h, kf.h, cols[:, C_INVF:C_INVF + 1], ALU.mult, [kf.r(), env.cres], [ang.r()],
               s2=cols[:, C_PHASE:C_PHASE + 1], op1=ALU.add)
            STT(P, ang.h, kf.h, cols[:, C_INVFLO:C_INVFLO + 1], ang.h, ALU.mult, ALU.add, [kf.r(), ang.r(), env.cres], [ang.r()])
            TS(P, "dve", kf.h, ang.h, 1.0 / (2 * math.pi), ALU.mult, [ang.r()], [kf.r()])
            CP(P, "dve", ki_i, kf.h, [kf.r()], [ki.r()])

        def c2():
            CP(P, "dve", kf.h, ki_i, [ki.r()], [kf.r()])
            C1 = 6.28125
            C2 = 2 * math.pi - C1
            STT(P, ang.h, kf.h, -C1, ang.h, ALU.mult, ALU.add, [kf.r(), ang.r()], [ang.r()])
            STT(P, ang.h, kf.h, -C2, ang.h, ALU.mult, ALU.add, [kf.r(), ang.r()], [ang.r()])
            TS(P, "dve", ang.h, ang.h, 3.1415925, ALU.min, [ang.r()], [ang.r()], s2=-3.1415925, op1=ALU.max)

        def c3():
            ACT(P, cs.h[:], ang.h, AF.Sin, [ang.r()], [cs.r()])
        return [c0, c1, c2, c3]

    def front_preload(b, g):
        for i in range(2):
            load_x_tile(x_d[b, g * TG + i * 128: g * TG + (i + 1) * 128, :], i)

    def front_chunks(b, g, hT):
        t0 = g * TG
        junk, ss, lnv, rstd, hn = stage["junk"], stage["ss"], stage["lnv"], stage["rstd"], stage["hn"]
        chunks = []
        for half in range(2):
            grp = [half * 2, half * 2 + 1]

            def stats(half=half, grp=grp):
                for i, tb in enumerate(grp):
                    ACT(P, junk.h[:], xs[i].h[:], AF.Square, [xs[i].r()], [junk.r(), ss.r(tb)], accum_out=ss.h[:, tb:tb + 1])
                a, bnd = grp[0], grp[-1] + 1
                ACT(P, lnv.h[:, a:bnd], ss.h[:, a:bnd], AF.Ln, [ss.r(t) for t in grp], [lnv.r(t) for t in grp],
                    scale=1.0 / D, bias=EPS)
                ACT(P, rstd.h[:, a:bnd], lnv.h[:, a:bnd], AF.Exp, [lnv.r(t) for t in grp], [rstd.r(t) for t in grp],
                    scale=-0.5)

            def scale_(half=half, grp=grp):
                for i, tb in enumerate(grp):
                    TS(P, "dve", hn.h[:, tb, :], xs[i].h[:], rstd.h[:, tb:tb + 1], ALU.mult, [xs[i].r(), rstd.r(tb)], [hn.r(tb)])
                if half == 0:
                    for i in range(2):
                        load_x_tile(x_d[b, t0 + (2 + i) * 128: t0 + (3 + i) * 128, :], i)
            chunks += [stats, scale_]
        for kp in range(4):
            def tr(kp=kp):
                bank = env.pst[:, kp % 2, :]
                bres = env.tbank[kp % 2]
                for kk in range(2):
                    k = 2 * kp + kk
                    for tb in range(4):
                        TR(P, bank[:, kk * 512 + tb * 128: kk * 512 + (tb + 1) * 128], hn.h[:, tb, k * 128:(k + 1) * 128],
                           env.ident, [hn.r(tb), env.cres], [bres])

            def ev(kp=kp):
                bank = env.pst[:, kp % 2, :]
                bres = env.tbank[kp % 2]
                for kk in range(2):
                    k = 2 * kp + kk
                    src = bank[:, kk * 512:(kk + 1) * 512]
                    gsc = cols[:, C_GPRE + k: C_GPRE + k + 1]
                    if kk == 0:
                        TS(P, "dve", hT.h[:, k, :], src, gsc, ALU.mult, [bres, env.cres], [hT.r()])
                    else:
                        ACT(P, hT.h[:, k, :], src, AF.Copy, [bres, env.cres], [hT.r()], scale=gsc)
            chunks += [tr, ev]
        return chunks

    hT_bufs = [hT, hT2]
    cs_bufs = [cs, cs2]

    def prologue_chunks(b):
        ch = [load_small_weights] if b == 0 else []
        ch += mem_chunks(b)
        ch += [lambda: front_preload(b, 0)]
        if b == 0:
            ch += rope_chunks(b, 0, cs_bufs[0])
        ch += front_chunks(b, 0, hT_bufs[0])
        return ch

    for ch in prologue_chunks(0):
        ch()
    for b, g in [(b_, g_) for b_ in range(NBC) for g_ in range(NG)]:
        t0 = g * TG
        hT = hT_bufs[g % 2]
        cs = cs_bufs[g % 2]

        def consume(c, bk, br):
            if c < 3:
                CP(P, "dve", cq.h[:, c, :], bk, [br], [cq.r(c)])
                ACT(P, sq[c % 3].h[:], bk, AF.Square, [br], [sq[c % 3].r()])
                if c == 2:
                    sb_, sr_ = banks.get()
                    for j in range(3):
                        MM(P, sb_, env.ones, sq[j].h[:], j == 0, j == 2, [sq[j].r(), env.cres], [sr_])
                    rstd_from_psum(P, env, sb_, sr_, 384, lntmp, rq, slice(0, TG))
                    for j in range(3):
                        STT(P, cq.h[:, j, :], cq.h[:, j, :], cols[:, C_GQ + j:C_GQ + j + 1], rq.h[:], ALU.mult, ALU.mult,
                            [cq.r(j), rq.r(), env.cres], [cq.r(j)])
            elif c < 5:
                j = c - 3
                CP(P, "dve", ckv.h[:, j, :], bk, [br], [ckv.r(j)])
                ACT(P, sq[j].h[:], bk, AF.Square, [br], [sq[j].r()])
                if j == 1:
                    sb_, sr_ = banks.get()
                    for jj in range(2):
                        MM(P, sb_, env.ones, sq[jj].h[:], jj == 0, jj == 1, [sq[jj].r(), env.cres], [sr_])
                    rstd_from_psum(P, env, sb_, sr_, 256, lntmp, rkv, slice(0, TG))
                    for jj in range(2):
                        STT(P, ckv.h[:, jj, :], ckv.h[:, jj, :], cols[:, C_GKV + jj:C_GKV + jj + 1], rkv.h[:], ALU.mult,
                            ALU.mult, [ckv.r(jj), rkv.r(), env.cres], [ckv.r(jj)])
            elif c == 5:
                TT(P, "dve", yk.h[:], bk, cs.h[:], ALU.mult, [br, cs.r()], [yk.r()])
            elif c < 10:
                j = c - 6
                ACT(P, sbq.h[:, j, :], bk, AF.Copy, [br], [sbq.r((j, 0)), sbq.r((j, 1))], scale=0.125)
            elif c < 14:
                j = c - 10
                CP(P, "dve", sbk.h[:, j, t0:t0 + TG], bk, [br], [sbk.r((j, g))])
            else:
                j = c - 14
                ACT(P, qm.h[:, j, :], bk, AF.Copy, [br], [qm.r(j)])
        if g + 1 < NG:
            rope_next = rope_chunks(b, g + 1, cs_bufs[(g + 1) % 2])
        elif b + 1 < NBC:
            rope_next = rope_chunks(b + 1, 0, cs_bufs[0])
        else:
            rope_next = []
        if rope_next:
            rope_next[0]()
        project_chunks([w1_d[c] for c in range(18)], consume, TG, hT=hT)
        for ch in rope_next[1:]:
            ch()
        project_v(w1v_d, 4, sbv, g * 4, hT=hT)

        for h in range(8):
            bk, br = banks.get()
            for k in range(3):
                MM(P, bk, wuq.h[:, k, h * 128:(h + 1) * 128], cq.h[:, k, :], k == 0, k == 2, [wuq.r(), cq.r(k)], [br])
            TT(P, "dve", qaug.h[:, h, :], bk, cs.h[:], ALU.mult, [br, cs.r()], [qaug.r(h)])
        for h in range(8):
            bk, br = banks.get()
            for k in range(2):
                MM(P, bk, wuk.h[:, k, h * 128:(h + 1) * 128], ckv.h[:, k, :], k == 0, False, [wuk.r(), ckv.r(k)], [br])
            MM(P, bk, env.f2, yk.h[:], False, True, [yk.r(), env.cres], [br])
            ACT(P, kaug.h[:, h, t0:t0 + TG], bk, AF.Copy, [br], [kaug.r((h, g))])
        for tb in range(4):
            bk, br = banks.get()
            for k in range(2):
                MM(P, bk, ckv.h[:, k, tb * 128:(tb + 1) * 128], wuv.h[:, k, :], k == 0, k == 1, [ckv.r(k), wuv.r()], [br])
            CP(P, "dve", vmla.h[:, g * 4 + tb, :], bk, [br], [vmla.r(g * 4 + tb)])

        nkb = 4 * g + 4
        abanks = Rot([(ps[:, i, :], env.bank[i]) for i in range(8)])
        pbanks = Rot([(ps[:, 2 * i:2 * i + 2, :], [env.bank[2 * i], env.bank[2 * i + 1]]) for i in range(3)])
        obanks = Rot([(ps[:, 6 + i, :], env.bank[6 + i]) for i in range(2)])
        mla_s = Rot([(ps[:, i, :], env.bank[i]) for i in range(4)])
        mla_o = Rot([(ps[:, 4 + i, :], env.bank[4 + i]) for i in range(2)])

        MSCALE = 128 ** -0.5
        mtasks = []
        for h in range(4):
            mtasks.append(mem_task(P, env, h, MSCALE, memK, memV, qm, aT[h % 2], rec[h % 2], ef[h % 2],
                                   [abanks.get() for _ in range(4)]))
        run_pipeline(mtasks, [("Z", 1), ("PV", 0), ("N", -1)])
        P.dma("sp", O_d[b, 8:12, :, t0:t0 + TG].rearrange("c p t -> p c t"), qm.h[:], ost_slot[2],
              reads=[qm.r(j) for j in range(4)])

        if g + 1 < NG:
            front_preload(b, g + 1)
        tasks = []
        ti = 0
        for hp in range(4):
            ob, obr = obanks.get()
            ls = lsum[hp % 2]
            for idx, kb in enumerate(range(nkb - 1, -1, -1)):
                tasks.append(sb_task(P, env, g, hp, idx, kb, nkb, ti, ob, obr, ls, sbq, sbk, sbv, pbanks.get(), ef, lp, aT))
                ti += 1
        run_pipeline(tasks, [("Z", 1), ("A", -1), ("L", 0), ("E", 1), ("CUMA", 0), ("AV", -1), ("CUMB", 0)])
        P.dma("sp", O_d[b, 4:8, :, t0:t0 + TG].rearrange("c p t -> p c t"), sbq.h[:], ost_slot[1],
              reads=[sbq.r((j, i)) for j in range(4) for i in range(2)])

        extras = []
        if g + 1 < NG:
            extras = front_chunks(b, g + 1, hT_bufs[(g + 1) % 2])
        elif b + 1 < NBC:
            extras = prologue_chunks(b + 1)

        tasks = []
        ti = 0
        for h in range(8):
            ob, obr = mla_o.get()
            for kb in range(nkb):
                tasks.append(mla_task(P, env, g, h, kb, nkb, ti, ob, obr, kaug, qaug, vmla, vaug[h % 2], omla, rec[h % 2],
                                      mla_s.get(), aT, lt=ef[1] if (g <= 1 and h % 2 == 1) else None))
                ti += 1
        run_pipeline(tasks, [("Z", 1), ("P", 0), ("PV", -1)], extras=extras,
                     every=max(1, len(tasks) // (len(extras) + 1)))
        P.dma("sp", O_d[b, 0:4, :, t0:t0 + TG].rearrange("c p t -> p c t"), omla.h[:], ost_slot[0],
              reads=[omla.r((j, i)) for j in range(4) for i in range(2)])


def mem_task(P, env, h, scale, memK, memV, qm, a2, rc, lt, bks):
    (zb0, zr0), (zb1, zr1), (ob, obr), (db, dbr) = bks
    zs = [(zb0, zr0), (zb1, zr1)]

    def fZ():
        for mt in range(2):
            MM(P, zs[mt][0], memK.h[:, h, mt * 128:(mt + 1) * 128], qm.h[:, h, :], True, True, [memK.r(), qm.r(h)], [zs[mt][1]])
        for mt in range(2):
            ACT(P, a2.h[:, mt, :], zs[mt][0], AF.Exp, [zs[mt][1]], [a2.r()], scale=scale)

    def fPV():
        for mt in range(2):
            MM(P, ob, memV.h[:, mt, h * 128:(h + 1) * 128], a2.h[:, mt, :], mt == 0, mt == 1, [memV.r(mt), a2.r()], [obr])
        for mt in range(2):
            MM(P, db, env.ones, a2.h[:, mt, :], mt == 0, mt == 1, [a2.r(), env.cres], [dbr])

    def fN():
        ACT(P, lt.h[:, 0, :], db, AF.Ln, [dbr], [lt.r()])
        ACT(P, rc.h[:], lt.h[:, 0, :], AF.Exp, [lt.r()], [rc.r()], scale=-1.0)
        TT(P, "dve", qm.h[:, h, :], ob, rc.h[:], ALU.mult, [obr, rc.r()], [qm.r(h)])

    return {"Z": fZ, "PV": fPV, "N": fN}


def run_pipeline(tasks, order, extras=(), every=3):
    n = len(tasks)
    lo = min(off for _, off in order)
    hi = max(off for _, off in order)
    extras = list(extras)
    for cnt, s_ in enumerate(range(-hi, n - lo)):
        for name, off in order:
            i = s_ + off
            if 0 <= i < n:
                tasks[i][name]()
        if extras and cnt % every == every - 1:
            extras.pop(0)()
    for ex in extras:
        ex()


def sb_task(P, env, g, hp, idx, kb, nkb, ti, ob, obr, ls, sbq, sbk, sbv, zbank, ef, lp, aT):
    j = hp
    jd = kb - 4 * g
    c0 = 128 * jd if jd >= 0 else 0
    kres = sbk.r((j, kb // 4))
    zb2, zr = zbank
    e_, l_, a = ef[ti % 2], lp[ti % 2], aT[ti % 3]

    def fZ():
        for i in range(2):
            po = 64 * i
            MM(P, zb2[:, i, c0:TG], sbk.h[po:po + 64, j, kb * 128:(kb + 1) * 128], sbq.h[po:po + 64, j, c0:TG], True, jd < 0,
               [kres, sbq.r((j, i))], [zr[i]])
        if jd >= 0:
            for i in range(2):
                MM(P, zb2[:, i, c0:c0 + 128], env.ident, env.mneg_sb, False, True, [env.cres], [zr[i]])

    def fE():
        ACT(P, e_.h[:, :, c0:TG], zb2[:, :, c0:TG], AF.Exp, zr, [e_.r()])

    def fL():
        ACT(P, l_.h[:, :, c0:TG], e_.h[:, :, c0:TG], AF.Ln, [e_.r()], [l_.r()], bias=1.0)

    def fCUMA():
        if idx == 0:
            MSET(P, "pool", ls.h[:], 0.0, [ls.r()])
        if idx > 0:
            c1 = 128 * (jd + 1) if jd >= 0 else 0
            for i in range(2):
                MM(P, zb2[:, i, c1:TG], env.negones, ls.h[:, i, c1:TG], False, True, [ls.r(), env.cres], [zr[i]], skip=True)

    def fCUMB():
        for i in range(2):
            MM(P, zb2[:, i, c0:TG], env.tneg, l_.h[:, i, c0:TG], False, True, [l_.r(), env.cres], [zr[i]], skip=True)
        if kb > 0:
            TT(P, "dve", ls.h[:, :, c0:TG], ls.h[:, :, c0:TG], l_.h[:, :, c0:TG], ALU.add, [ls.r(), l_.r()], [ls.r()])

    def fA():
        ACT(P, a.h[:, :, c0:TG], zb2[:, :, c0:TG], AF.Exp, zr, [a.r()])

    def fAV():
        for i in range(2):
            po = 64 * i
            h = 2 * hp + i
            MM(P, ob[po:po + 64, c0:TG], sbv.h[:, kb, h * 64:(h + 1) * 64], a.h[:, i, c0:TG], idx == 0, kb == 0,
               [sbv.r(kb), a.r()], [obr], skip=True)
        if kb == 0:
            CP(P, "dve", sbq.h[:, j, :], ob, [obr], [sbq.r((j, 0)), sbq.r((j, 1))])

    return {"Z": fZ, "E": fE, "L": fL, "CUMA": fCUMA, "CUMB": fCUMB, "A": fA, "AV": fAV}


def mla_task(P, env, g, h, kb, nkb, ti, ob, obr, kaug, qaug, vmla, va, omla, rc, zbank, aT, lt=None):
    ASCALE = 96 ** -0.5
    j, po = h // 2, (h % 2) * 64
    jd = kb - 4 * g
    c0 = 128 * jd if jd >= 0 else 0
    zb, zr = zbank
    a3 = aT[ti % 3]
    a_res = a3.r()

    class a:
        h = a3.h[:, 0, :]

        @staticmethod
        def r():
            return a_res

    def fZ():
        if kb == 0:
            CP(P, "pool", va.h[:, 0:nkb, 0:64], vmla.h[:, 0:nkb, h * 64:(h + 1) * 64],
               [vmla.r(t) for t in range(nkb)], [va.r("v")])
        MM(P, zb[:, c0:TG], kaug.h[:, h, kb * 128:(kb + 1) * 128], qaug.h[:, h, c0:TG], True, jd < 0,
           [kaug.r((h, kb // 4)), qaug.r(h)], [zr])
        if jd >= 0:
            MM(P, zb[:, c0:c0 + 128], env.ident, env.mneg_mla, False, True, [env.cres], [zr])

    def fP():
        ACT(P, a.h[:, c0:TG], zb[:, c0:TG], AF.Exp, [zr], [a.r()], scale=ASCALE)

    def fPV():
        MM(P, ob[:, c0:TG], va.h[:, kb, :], a.h[:, c0:TG], kb == 0, kb == nkb - 1,
           [va.r("v"), va.r("ones"), a.r()], [obr], skip=True)
        if kb == nkb - 1:
            if lt is not None:
                ACT(P, lt.h[0:64, 0, :], ob[64:128, :], AF.Ln, [obr], [lt.r()])
                ACT(P, rc.h[0:64, :], lt.h[0:64, 0, :], AF.Exp, [lt.r()], [rc.r()], scale=-1.0)
            else:
                RECIP(P, rc.h[0:64, :], ob[64:128, :], [obr], [rc.r()])
            TT(P, "dve", omla.h[po:po + 64, j, :], ob[0:64, :], rc.h[0:64, :], ALU.mult, [obr, rc.r()],
               [omla.r((j, h % 2))])

    return {"Z": fZ, "P": fP, "PV": fPV}


def phase_C(P, nc, sc, env, x_d, O_d, wg_d, wb_d, wout_d, gpost_d, out_d, sbuf):
    ps = env.ps
    cols = env.cols
    wg = sbuf(sc, "wg", [128, 24, 8, 128], BF16)
    wb = sbuf(sc, "wbr", [128, 24, 4, 128], BF16)
    wout = sbuf(sc, "wout", [128, 8, 1024], BF16)
    gbc = sbuf(sc, "gbc", [128, 1024], F32)
    og = [sbuf(sc, "og%d" % i, [128, 12, TG], BF16) for i in range(2)]
    og_slot = [P.slot("og%d" % i) for i in range(2)]
    xr = [sbuf(sc, "xr%d" % i, [128, 4, 1024], F32) for i in range(2)]
    xr_slot = [P.slot("xr%d" % i) for i in range(2)]
    st_slot = [P.slot("stc%d" % i) for i in range(2)]
    stage = dict(junk=sbuf(sc, "junkc", [128, 1024], BF16), ss=sbuf(sc, "ssc", [128, 4], F32),
                 lnv=sbuf(sc, "lnvc", [128, 4], F32), rstd=sbuf(sc, "rstdc", [128, 4], F32),
                 hn=sbuf(sc, "hnc", [128, 4, 1024], BF16))
    hT = sbuf(sc, "hTc", [128, 8, TG], BF16)
    merged = sbuf(sc, "merged", [128, 8, TG], BF16)
    gt = [[sbuf(sc, "gt%d_%d" % (r, i), [128, TG], F32) for i in range(3)] for r in range(2)]
    ssy = sbuf(sc, "ssy", [128, 4], F32)
    lny = sbuf(sc, "lny", [128, 4], F32)
    rsy = sbuf(sc, "rsy", [128, 4], F32)
    tt = [sbuf(sc, "tt%d" % i, [128, 1024], F32) for i in range(2)]

    wslots = [P.slot("wc%d" % i) for i in range(17)]
    def load_weights(first_dep):
        for e in range(8):
            P.dma("pool", wg.h[:, e * 3:(e + 1) * 3, :, :], wg_d[e * 3:(e + 1) * 3].rearrange("c p k m -> p c k m"),
                  wslots[e], reads=first_dep if e == 0 else (), writes=[wg.r(e)])
            P.dma("pool", wb.h[:, e * 3:(e + 1) * 3, :, :], wb_d[e].rearrange("i p k m -> p i k m"), wslots[8 + e],
                  writes=[wb.r(e)])
            if e == 1:
                P.dma("pool", wout.h[:], wout_d, wslots[16], writes=[wout.r()])
    s_g = P.slot("gbc")

    banks = Rot([(ps[:, i, :], env.bank[i]) for i in range(6)])
    ybanks = Rot([(ps[:, 2 * i:2 * i + 2, :], (env.bank[2 * i], env.bank[2 * i + 1])) for i in range(3)])
    groups = [(b, g) for b in range(NBC) for g in range(NG)]

    def issue_loads(n):
        b, g = groups[n]
        t0 = g * TG
        P.dma("sp", og[n % 2].h[:], O_d[b, :, :, t0:t0 + TG].rearrange("c p t -> p c t"), og_slot[n % 2],
              writes=[og[n % 2].r()])
        P.dma("sp", xr[n % 2].h[:], x_d[b, t0:t0 + TG, :].rearrange("(t p) d -> p t d", p=128), xr_slot[n % 2],
              writes=[xr[n % 2].r(t) for t in range(4)])

    def prep(n):
        xb_ = xr[n % 2]
        return norm_transpose_chunks(P, env, [(xb_.h[:, t, :], xb_.r(t)) for t in range(4)], C_GPRE, hT.h, hT.r(),
                                     stage, [[0, 1, 2, 3]])

    issue_loads(0)
    P.dma("sp", gbc.h[:], gpost_d[0:1, :].partition_broadcast(128), s_g, writes=[gbc.r()])
    load_weights([xr[0].r(t) for t in range(4)])
    for n, (b, g) in enumerate(groups):
        t0 = g * TG
        if n + 1 < len(groups):
            issue_loads(n + 1)
        xb = xr[n % 2]
        ogb = og[n % 2]
        if n == 0:
            for ch in prep(0):
                ch()
        pchunks = prep(n + 1) if n + 1 < len(groups) else []
        for e in range(8):
            if e == 3 and pchunks:
                pchunks[0]()
            gts = gt[e % 2]
            for i in range(3):
                bk, br = banks.get()
                c = i * 8 + e
                for k in range(8):
                    MM(P, bk, wg.h[:, e * 3 + i, k, :], hT.h[:, k, :], k == 0, k == 7, [wg.r(e), hT.r()], [br])
                ACT(P, gts[i].h[:], bk, AF.Sigmoid, [br, env.cres], [gts[i].r()], bias=cols[:, C_BG + c:C_BG + c + 1])
                bk2, br2 = banks.get()
                for k in range(4):
                    MM(P, bk2, wb.h[:, e * 3 + i, k, :], ogb.h[:, i * 4 + k, :], k == 0, k == 3, [wb.r(e), ogb.r()], [br2])
                TT(P, "dve", gts[i].h[:], bk2, gts[i].h[:], ALU.mult, [br2, gts[i].r()], [gts[i].r()])
            TT(P, "pool", gts[0].h[:], gts[0].h[:], gts[1].h[:], ALU.add, [gts[0].r(), gts[1].r()], [gts[0].r()])
            TT(P, "pool", merged.h[:, e, :], gts[0].h[:], gts[2].h[:], ALU.add, [gts[0].r(), gts[2].r()], [merged.r()])
        for ch in pchunks[1:]:
            ch()
        for tb in range(4):
            yb, (yr0, yr1) = ybanks.get()
            for half in range(2):
                for k in range(8):
                    MM(P, yb[:, half, :], merged.h[:, k, tb * 128:(tb + 1) * 128], wout.h[:, k, half * 512:(half + 1) * 512],
                       k == 0, k == 7, [merged.r(), wout.r()], [yr0 if half == 0 else yr1])
            ACT(P, stage["junk"].h[:].rearrange("p (a b) -> p a b", a=2), yb, AF.Square, [yr0, yr1], [stage["junk"].r(), ssy.r(tb)],
                accum_out=ssy.h[:, tb:tb + 1])
            ACT(P, lny.h[:, tb:tb + 1], ssy.h[:, tb:tb + 1], AF.Ln, [ssy.r(tb)], [lny.r(tb)], scale=1.0 / D, bias=EPS)
            ACT(P, rsy.h[:, tb:tb + 1], lny.h[:, tb:tb + 1], AF.Exp, [lny.r(tb)], [rsy.r(tb)], scale=-0.5)
            t_ = tt[tb % 2]
            for half in range(2):
                STT(P, t_.h[:, half * 512:(half + 1) * 512], yb[:, half, :], rsy.h[:, tb:tb + 1],
                    gbc.h[:, half * 512:(half + 1) * 512], ALU.mult, ALU.mult,
                    [yr0 if half == 0 else yr1, rsy.r(tb), gbc.r()], [t_.r()])
            TT(P, "pool", xb.h[:, tb, :], xb.h[:, tb, :], t_.h[:], ALU.add, [xb.r(tb), t_.r()], [xb.r(tb)])
        P.dma("sp", out_d[b, t0:t0 + TG, :].rearrange("(t p) d -> p t d", p=128), xb.h[:], st_slot[n % 2],
              reads=[xb.r(t) for t in range(4)])


def phase_D(P, nc, sd, env, wup_d, wdn_d, gpost_d, out_d, sbuf):
    ps = env.ps
    wup = sbuf(sd, "wup", [128, 8, 4096], BF16)
    wdn = sbuf(sd, "wdn", [128, 32, 1024], BF16)
    gbc = sbuf(sd, "gbd", [128, 1024], F32)
    UT = 256
    x1 = [sbuf(sd, "x1_%d" % i, [128, 2, 1024], F32) for i in range(2)]
    x1_slot = [P.slot("x1_%d" % i) for i in range(2)]
    st_slot = [P.slot("std%d" % i) for i in range(2)]
    stage = dict(junk=sbuf(sd, "junkd", [128, 1024], BF16), ss=sbuf(sd, "ssd", [128, 4], F32),
                 lnv=sbuf(sd, "lnvd", [128, 4], F32), rstd=sbuf(sd, "rstdd", [128, 4], F32),
                 hn=sbuf(sd, "hnd", [128, 2, 1024], BF16))
    h2T = [sbuf(sd, "h2T%d" % i, [128, 8, UT], BF16) for i in range(2)]
    rl = [sbuf(sd, "rl%d" % i, [128, UT], BF16) for i in range(3)]
    uT = [sbuf(sd, "uT%d" % i, [128, UT], BF16) for i in range(8)]
    ssz = sbuf(sd, "ssz", [128, 4], F32)
    ssz2 = sbuf(sd, "ssz2", [128, 2], F32)
    lnz = sbuf(sd, "lnz", [128, 2], F32)
    rsz = sbuf(sd, "rsz", [128, 2], F32)
    tt = [sbuf(sd, "ttd%d" % i, [128, 1024], F32) for i in range(2)]

    wslots = [P.slot("wd%d" % i) for i in range(16)]
    def load_weights(first_dep):
        for q in range(8):
            P.dma("pool", wup.h[:, :, q * 512:(q + 1) * 512], wup_d[:, :, q * 512:(q + 1) * 512], wslots[q],
                  reads=first_dep if q == 0 else (), writes=[wup.r(q)])
            P.dma("pool", wdn.h[:, q * 4:(q + 1) * 4, :], wdn_d[:, q * 4:(q + 1) * 4, :], wslots[8 + q], writes=[wdn.r(q)])
    s_g = P.slot("gbd")

    zbank = [(ps[:, i, :], env.bank[i]) for i in range(4)]
    ubanks = Rot([(ps[:, 4 + i, :], env.bank[4 + i]) for i in range(2)])
    units = [(b, u) for b in range(NBC) for u in range(S // UT)]

    def issue_load(n):
        b, u = units[n]
        t0 = u * UT
        P.dma("sp", x1[n % 2].h[:], out_d[b, t0:t0 + UT, :].rearrange("(t p) d -> p t d", p=128), x1_slot[n % 2],
              writes=[x1[n % 2].r(t) for t in range(2)])

    def prep(n):
        xb_ = x1[n % 2]
        return norm_transpose_chunks(P, env, [(xb_.h[:, t, :], xb_.r(t)) for t in range(2)], C_GMLP, h2T[n % 2].h,
                                     h2T[n % 2].r(), stage, [[0, 1]])

    DEPTH = 3
    NU = DEPTH + 2
    issue_load(0)
    P.dma("sp", gbc.h[:], gpost_d[1:2, :].partition_broadcast(128), s_g, writes=[gbc.r()])
    load_weights([x1[0].r(t) for t in range(2)])
    for ch in prep(0):
        ch()
    for n, (b, u) in enumerate(units):
        t0 = u * UT
        if n + 1 < len(units):
            issue_load(n + 1)
        xb = x1[n % 2]
        hcur = h2T[n % 2]

        def up(f):
            ub, ur = ubanks.get()
            for k in range(8):
                MM(P, ub[:, 0:UT], wup.h[:, k, f * 128:(f + 1) * 128], hcur.h[:, k, :], k == 0, k == 7,
                   [wup.r(f // 4), hcur.r()], [ur])
            r_ = rl[f % 3]
            u_ = uT[f % NU]
            ACT(P, r_.h[:], ub[:, 0:UT], AF.Relu, [ur], [r_.r()])
            TT(P, "dve", u_.h[:], r_.h[:], r_.h[:], ALU.mult, [r_.r()], [u_.r()])

        def down(f):
            u_ = uT[f % NU]
            for tb in range(2):
                for half in range(2):
                    zb, zr = zbank[tb * 2 + half]
                    MM(P, zb, u_.h[:, tb * 128:(tb + 1) * 128], wdn.h[:, f, half * 512:(half + 1) * 512], f == 0, f == 31,
                       [u_.r(), wdn.r(f // 4)], [zr])

        pchunks = prep(n + 1) if n + 1 < len(units) else []
        sched = {8: 0, 18: 1, 21: 2, 24: 3, 27: 4}
        for f in range(32 + DEPTH):
            if f < 32:
                up(f)
            if f in sched and pchunks:
                pchunks[sched[f]]()
            if f - DEPTH >= 0:
                down(f - DEPTH)
        for tb in range(2):
            for half in range(2):
                zb, zr = zbank[tb * 2 + half]
                ACT(P, stage["junk"].h[:, 0:512], zb, AF.Square, [zr], [stage["junk"].r(), ssz.r(tb * 2 + half)],
                    accum_out=ssz.h[:, tb * 2 + half: tb * 2 + half + 1])
            TT(P, "dve", ssz2.h[:, tb:tb + 1], ssz.h[:, 2 * tb:2 * tb + 1], ssz.h[:, 2 * tb + 1:2 * tb + 2], ALU.add,
               [ssz.r(2 * tb), ssz.r(2 * tb + 1)], [ssz2.r(tb)])
            ACT(P, lnz.h[:, tb:tb + 1], ssz2.h[:, tb:tb + 1], AF.Ln, [ssz2.r(tb)], [lnz.r(tb)], scale=1.0 / D, bias=EPS)
            ACT(P, rsz.h[:, tb:tb + 1], lnz.h[:, tb:tb + 1], AF.Exp, [lnz.r(tb)], [rsz.r(tb)], scale=-0.5)
            t_ = tt[tb % 2]
            for half in range(2):
                zb, zr = zbank[tb * 2 + half]
                STT(P, t_.h[:, half * 512:(half + 1) * 512], zb, rsz.h[:, tb:tb + 1], gbc.h[:, half * 512:(half + 1) * 512],
                    ALU.mult, ALU.mult, [zr, rsz.r(tb), gbc.r()], [t_.r()])
            TT(P, "pool", xb.h[:, tb, :], xb.h[:, tb, :], t_.h[:], ALU.add, [xb.r(tb), t_.r()], [xb.r(tb)])
        P.dma("sp", out_d[b, t0:t0 + UT, :].rearrange("(t p) d -> p t d", p=128), xb.h[:], st_slot[n % 2],
              reads=[xb.r(t) for t in range(2)])


def _chunkify(W):
    C = W.shape[1]
    return np.ascontiguousarray(W.reshape(8, 128, C // 128, 128).transpose(2, 1, 0, 3))


def _rows(W, nk):
    return np.ascontiguousarray(W.reshape(nk, 128, W.shape[1]).transpose(1, 0, 2))


def prep_shared(inp):
    f = np.float32
    w_in = np.asarray(inp["w_in"], f)[0]
    kr = np.zeros((D, 128), f)
    kr[:, 64:96] = w_in[:, 640:672]
    kr[:, 96:112] = w_in[:, 656:672]
    kr[:, 112:128] = w_in[:, 640:656]
    w1cat = np.concatenate([w_in[:, 0:640], kr, w_in[:, 672:1184], w_in[:, 1184:1696], w_in[:, 2208:2720]], axis=1)
    sh = {}
    sh["w1"] = _chunkify(w1cat)
    sh["w1v"] = _rows(w_in[:, 1696:2208], 8)
    wmkv = np.asarray(inp["w_mem_kv"], f)[0]
    sh["wmk"] = _chunkify(wmkv[:, 0:512])
    sh["wmv"] = _chunkify(wmkv[:, 512:1024])
    w_uq = np.asarray(inp["w_uq"], f)[0]
    wq = np.zeros((384, 1024), f)
    for h in range(8):
        s = h * 96
        wq[:, h * 128:h * 128 + 64] = w_uq[:, s:s + 64]
        wq[:, h * 128 + 64:h * 128 + 96] = w_uq[:, s + 64:s + 96]
        wq[:, h * 128 + 96:h * 128 + 112] = w_uq[:, s + 80:s + 96]
        wq[:, h * 128 + 112:h * 128 + 128] = w_uq[:, s + 64:s + 80]
    sh["wuq"] = _rows(wq, 3)
    w_uk = np.asarray(inp["w_uk"], f)[0]
    wk = np.zeros((256, 1024), f)
    for h in range(8):
        wk[:, h * 128:h * 128 + 64] = w_uk[:, h * 64:(h + 1) * 64]
    sh["wuk"] = _rows(wk, 2)
    sh["wuv"] = _rows(np.asarray(inp["w_uv"], f)[0], 2)
    wgc = _chunkify(w_in[:, 2720:5792])
    sh["wg"] = np.ascontiguousarray(wgc.reshape(3, 8, 128, 8, 128).transpose(1, 0, 2, 3, 4).reshape(24, 128, 8, 128))
    wbo = np.asarray(inp["w_branch_out"], f)[0]
    sh["wb"] = np.ascontiguousarray(wbo.reshape(3, 4, 128, 8, 128).transpose(3, 0, 2, 1, 4))
    sh["wout"] = _rows(np.asarray(inp["w_out"], f)[0], 8)
    sh["wup"] = _rows(np.asarray(inp["w_mlp_up"], f)[0], 8)
    sh["wdn"] = _rows(np.asarray(inp["w_mlp_down"], f)[0], 32)
    cols = np.zeros((128, NCOL), f)
    cols[:, C_GPRE:C_GPRE + 8] = np.asarray(inp["ln_mix_pre"], f)[0].reshape(8, 128).T
    cols[:, C_GMEM:C_GMEM + 8] = np.asarray(inp["mem_norm"], f)[0].reshape(8, 128).T
    cols[:, C_GMLP:C_GMLP + 8] = np.asarray(inp["ln_mlp_pre"], f)[0].reshape(8, 128).T
    cols[:, C_GQ:C_GQ + 3] = np.asarray(inp["q_norm"], f)[0].reshape(3, 128).T
    cols[:, C_GKV:C_GKV + 2] = np.asarray(inp["kv_norm"], f)[0].reshape(2, 128).T
    cols[:, C_BG:C_BG + 24] = np.asarray(inp["b_gate"], f)[0].reshape(24, 128).T
    invf64 = 1.0 / (10000.0 ** (np.arange(16, dtype=np.float64) * (2.0 / 32)))
    invf = invf64.astype(f)
    invf_lo = (invf64 - invf.astype(np.float64)).astype(f)
    ivl = np.zeros(128, f)
    ivl[64:80] = invf_lo
    ivl[80:96] = invf_lo
    ivl[96:112] = -invf_lo
    ivl[112:128] = invf_lo
    cols[:, C_INVFLO] = ivl
    iv = np.zeros(128, f)
    ph = np.zeros(128, f)
    ph[0:96] = np.pi / 2
    iv[64:80] = invf
    iv[80:96] = invf
    iv[96:112] = -invf
    iv[112:128] = invf
    cols[:, C_INVF] = iv
    cols[:, C_PHASE] = ph
    sh["cols"] = cols
    sh["gpost"] = np.stack([np.asarray(inp["ln_mix_post"], f)[0], np.asarray(inp["ln_mlp_post"], f)[0]])
    jj = np.arange(128)[:, None]
    tt = np.arange(128)[None, :]
    cm = np.zeros((7, 128, 128), f)
    cm[0] = np.eye(128, dtype=f)
    cm[1] = 1.0
    cm[2] = np.where(jj >= tt, -1.0, 0.0)
    cm[3] = -1.0
    cm[4] = np.where(tt <= jj, MASKV, 0.0)
    cm[5] = np.where(tt < jj, MASKV, 0.0)
    for j in range(64):
        cm[6][64 + j, 64 + j % 32] = 1.0
        cm[6][64 + j, 96 + j % 32] = 1.0
    sh["cmat"] = np.ascontiguousarray(cm.transpose(1, 0, 2))
    return sh


_NC_CACHE = {}


def kernel(**inputs):
    x = np.ascontiguousarray(np.asarray(inputs["x"], np.float32))
    mem = np.ascontiguousarray(np.asarray(inputs["mem"], np.float32))
    pos = np.ascontiguousarray(np.asarray(inputs["positions"], np.int32))
    sh = prep_shared(inputs)
    if "nc" not in _NC_CACHE:
        _NC_CACHE["nc"] = build_program()
    nc = _NC_CACHE["nc"]
    in_maps = []
    for c in range(NCORES):
        m = dict(sh)
        m["x"] = x[c * NBC:(c + 1) * NBC]
        m["mem"] = mem[c * NBC:(c + 1) * NBC]
        m["pos"] = pos[c * NBC:(c + 1) * NBC]
        in_maps.append(m)
    res = run_bass_kernel_spmd(nc, in_maps, core_ids=list(range(NCORES)))
    kernel.last_results = res
    return np.concatenate([np.asarray(r["out"], np.float32) for r in res.results], axis=0)
```

```python
import math
from contextlib import ExitStack
import numpy as np
import concourse.bass as bass
import concourse.mybir as mybir
from concourse.bass_utils import run_bass_kernel_spmd

F32 = mybir.dt.float32
BF16 = mybir.dt.bfloat16
I32 = mybir.dt.int32
AF = mybir.ActivationFunctionType
ALU = mybir.AluOpType

NCORES = 8
NBC = 2
S = 2048
D = 1024
TG = 512
NG = S // TG
EPS = 1e-6
MASKV = -30000.0
ENGS = ("pe", "act", "dve", "pool", "sp")

C_GPRE, C_GMEM, C_GMLP, C_GQ, C_GKV, C_BG, C_INVF, C_PHASE, C_INVFLO = 0, 8, 16, 24, 27, 29, 53, 54, 55
NCOL = 56
STOP_AFTER = None
DEBUG_SCRATCH = False


class Res:
    __slots__ = ("name", "last_w", "readers", "excl")

    def __init__(self, name, excl=False):
        self.name = name
        self.last_w = None
        self.readers = []
        self.excl = excl


class Slot:
    def __init__(self, name):
        self.name = name
        self.count = 0
        self.sem = None


class Op:
    __slots__ = ("eng", "fn", "deps", "slot", "val", "needs_inc", "is_dma", "done")

    def __init__(self, eng, fn):
        self.eng = eng
        self.fn = fn
        self.deps = []
        self.slot = None
        self.val = None
        self.needs_inc = False
        self.is_dma = False
        self.done = False


class Prog:
    def __init__(self, nc, es):
        self.nc = nc
        self.es = es
        self.streams = {e: [] for e in ENGS}
        self.count = {e: 0 for e in ENGS}
        self.waited = {e: {} for e in ENGS}
        self.esem = {e: es.enter_context(nc.semaphore("sem_" + e)) for e in ("pe", "act", "dve", "pool")}
        self.slots = []
        self.n_ops = 0

    def slot(self, name):
        s = Slot(name)
        s.sem = self.es.enter_context(self.nc.semaphore("sl_" + name))
        self.slots.append(s)
        return s

    def _dep(self, o, d, raw):
        if d is None or d is o or d.done:
            return
        if (not d.is_dma) and (not o.is_dma) and d.eng == o.eng:
            if o.eng == "pe" or not raw:
                return
        o.deps.append(d)

    def op(self, eng, fn, reads=(), writes=(), slot=None):
        o = Op(eng, fn)
        if slot is not None:
            o.is_dma = True
            o.slot = slot
            slot.count += 1
            o.val = 16 * slot.count
        for r in reads:
            self._dep(o, r.last_w, True)
            if r.excl:
                for x in r.readers:
                    if x.eng != eng or x.is_dma:
                        self._dep(o, x, False)
        for w in writes:
            self._dep(o, w.last_w, True)
            for x in w.readers:
                self._dep(o, x, False)
        for d in o.deps:
            d.needs_inc = True
        for r in reads:
            r.readers.append(o)
        for w in writes:
            w.last_w = o
            w.readers = []
        self.streams[eng].append(o)
        self.n_ops += 1
        return o

    def dma(self, queue, out, in_, slot, reads=(), writes=()):
        if queue == "pool":
            return self.op(queue, lambda e: e.dma_start(out=out, in_=in_, max_dma_last_dim=4096), reads, writes, slot=slot)
        return self.op(queue, lambda e: e.dma_start(out=out, in_=in_), reads, writes, slot=slot)

    def _emit_stream(self, eng, e):
        waited = self.waited[eng]
        for o in self.streams[eng]:
            best = {}
            for d in o.deps:
                if d.is_dma:
                    key, sem, val = ("s", id(d.slot)), d.slot.sem, d.val
                else:
                    key, sem, val = ("e", d.eng), self.esem[d.eng], d.val
                if val > best.get(key, (None, 0))[1]:
                    best[key] = (sem, val)
            for key, (sem, val) in best.items():
                if waited.get(key, 0) >= val:
                    continue
                waited[key] = val
                e.wait_ge(sem, val)
            inst = o.fn(e)
            if o.is_dma:
                inst.then_inc(o.slot.sem, 16)
            elif o.needs_inc:
                inst.then_inc(self.esem[eng], 1)
            o.done = True
        for x in ("pe", "act", "dve", "pool"):
            if x != eng and waited.get(("e", x), 0) < self.count[x]:
                waited[("e", x)] = self.count[x]
                e.wait_ge(self.esem[x], self.count[x])
        for sl in self.slots:
            key = ("s", id(sl))
            if sl.count and waited.get(key, 0) < 16 * sl.count:
                waited[key] = 16 * sl.count
                e.wait_ge(sl.sem, 16 * sl.count)

    def flush(self):
        nc = self.nc
        for eng in ENGS:
            comp = [o for o in self.streams[eng] if not o.is_dma]
            if comp:
                comp[-1].needs_inc = True
            for o in self.streams[eng]:
                if (not o.is_dma) and o.needs_inc:
                    self.count[eng] += 1
                    o.val = self.count[eng]
        with nc.Block() as block:
            @block.tensor
            def _(e):
                self._emit_stream("pe", e)

            @block.scalar
            def _(e):
                self._emit_stream("act", e)

            @block.vector
            def _(e):
                self._emit_stream("dve", e)

            @block.gpsimd
            def _(e):
                self._emit_stream("pool", e)

            @block.sync
            def _(e):
                self._emit_stream("sp", e)
        self.streams = {e: [] for e in ENGS}


class Buf:
    def __init__(self, h, name):
        self.h = h
        self.name = name
        self.res = {}

    def r(self, key=None):
        if key not in self.res:
            self.res[key] = Res("%s:%s" % (self.name, key))
        return self.res[key]


def MM(P, out, lhsT, rhs, start, stop, rd, wr, skip=False):
    P.op("pe", lambda e: e.matmul(out, lhsT, rhs, start=start, stop=stop, skip_group_check=skip), rd, wr)


def TR(P, out, in_, ident, rd, wr):
    P.op("pe", lambda e: e.transpose(out=out, in_=in_, identity=ident), rd, wr)


def ACT(P, out, in_, func, rd, wr, **kw):
    P.op("act", lambda e: e.activation(out=out, in_=in_, func=func, **kw), rd, wr)


def TS(P, eng, out, in0, s1, op0, rd, wr, s2=None, op1=None):
    if op1 is None:
        P.op(eng, lambda e: e.tensor_scalar(out=out, in0=in0, scalar1=s1, scalar2=None, op0=op0), rd, wr)
    else:
        P.op(eng, lambda e: e.tensor_scalar(out=out, in0=in0, scalar1=s1, scalar2=s2, op0=op0, op1=op1), rd, wr)


def TT(P, eng, out, in0, in1, op, rd, wr):
    P.op(eng, lambda e: e.tensor_tensor(out=out, in0=in0, in1=in1, op=op), rd, wr)


def STT(P, out, in0, scalar, in1, op0, op1, rd, wr):
    P.op("dve", lambda e: e.scalar_tensor_tensor(out=out, in0=in0, scalar=scalar, in1=in1, op0=op0, op1=op1), rd, wr)


def CP(P, eng, out, in_, rd, wr):
    P.op(eng, lambda e: e.tensor_copy(out=out, in_=in_), rd, wr)


def RECIP(P, out, in_, rd, wr):
    P.op("dve", lambda e: e.reciprocal(out=out, in_=in_), rd, wr)


def MSET(P, eng, ap, val, wr):
    P.op(eng, lambda e: e.memset(ap, val), (), wr)


class Env:
    pass


class Rot:
    def __init__(self, items):
        self.items = items
        self.i = 0

    def get(self):
        it = self.items[self.i % len(self.items)]
        self.i += 1
        return it


class WStream:
    uid = 0

    def __init__(self, P, nc, es, name, shape, nbuf):
        self.P = P
        self.bufs = []
        for i in range(nbuf):
            WStream.uid += 1
            h = es.enter_context(nc.sbuf_tensor("ws%d_%s%d" % (WStream.uid, name, i), shape, BF16))
            self.bufs.append((h, Res("%s%d" % (name, i)), P.slot("%s%d_%d" % (name, i, WStream.uid))))
        self.i = 0

    def run(self, srcs, body, depth=None, view=None):
        n = len(srcs)
        depth = depth or (len(self.bufs) - 1)
        base = self.i
        self.i += n

        def issue(j):
            h, r, sl = self.bufs[(base + j) % len(self.bufs)]
            dst = view(h) if view else h[:]
            self.P.dma("pool", dst, srcs[j], sl, writes=[r])

        for j in range(min(depth, n)):
            issue(j)
        for j in range(n):
            h, r, sl = self.bufs[(base + j) % len(self.bufs)]
            body(j, h, r)
            if j + depth < n:
                issue(j + depth)


def norm_transpose_chunks(P, env, tiles, gcol0, dst, dst_res, stage, tb_groups):
    ntb = len(tiles)
    junk, ss, lnv, rstd, hn = stage["junk"], stage["ss"], stage["lnv"], stage["rstd"], stage["hn"]

    def stats():
        for grp in tb_groups:
            a, bnd = grp[0], grp[-1] + 1
            for tb in grp:
                xa, xr = tiles[tb]
                ACT(P, junk.h[:], xa, AF.Square, [xr], [junk.r(), ss.r(tb)], accum_out=ss.h[:, tb:tb + 1])
            ACT(P, lnv.h[:, a:bnd], ss.h[:, a:bnd], AF.Ln, [ss.r(t) for t in grp], [lnv.r(t) for t in grp],
                scale=1.0 / D, bias=EPS)
            ACT(P, rstd.h[:, a:bnd], lnv.h[:, a:bnd], AF.Exp, [lnv.r(t) for t in grp], [rstd.r(t) for t in grp],
                scale=-0.5)
            for tb in grp:
                xa, xr = tiles[tb]
                TS(P, "dve", hn.h[:, tb, :], xa, rstd.h[:, tb:tb + 1], ALU.mult, [xr, rstd.r(tb)], [hn.r(tb)])

    chunks = [stats]
    for kp in range(4):
        def tr(kp=kp):
            bank = env.pst[:, kp % 2, :]
            bres = env.tbank[kp % 2]
            for kk in range(2):
                k = 2 * kp + kk
                for tb in range(ntb):
                    TR(P, bank[:, kk * 512 + tb * 128: kk * 512 + (tb + 1) * 128], hn.h[:, tb, k * 128:(k + 1) * 128],
                       env.ident, [hn.r(tb), env.cres], [bres])
            for kk in range(2):
                k = 2 * kp + kk
                src = bank[:, kk * 512: kk * 512 + ntb * 128]
                g = env.cols[:, gcol0 + k: gcol0 + k + 1]
                if kk == 0:
                    TS(P, "dve", dst[:, k, 0:ntb * 128], src, g, ALU.mult, [bres, env.cres], [dst_res])
                else:
                    ACT(P, dst[:, k, 0:ntb * 128], src, AF.Copy, [bres, env.cres], [dst_res], scale=g)
        chunks.append(tr)
    return chunks


def norm_transpose(P, env, tiles, gcol0, dst, dst_res, stage, tb_groups):
    for ch in norm_transpose_chunks(P, env, tiles, gcol0, dst, dst_res, stage, tb_groups):
        ch()


def rstd_from_psum(P, env, bank_ap, bank_res, n_feat, tmp, out, cols):
    ACT(P, tmp.h[:, cols], bank_ap, AF.Ln, [bank_res], [tmp.r()], scale=1.0 / n_feat, bias=EPS)
    ACT(P, out.h[:, cols], tmp.h[:, cols], AF.Exp, [tmp.r()], [out.r()], scale=-0.5)


def build_program():
    nc = bass.Bass("TRN2", target_bir_lowering=False)

    def din(name, shape, dt=F32):
        return nc.dram_tensor(name, list(shape), dt, kind="ExternalInput").ap()

    x_d = din("x", [NBC, S, D])
    mem_d = din("mem", [NBC, 256, D])
    pos_d = din("pos", [NBC, S], I32)
    w1_d = din("w1", [18, 128, 8, 128])
    w1v_d = din("w1v", [128, 8, 512])
    wmk_d = din("wmk", [4, 128, 8, 128])
    wmv_d = din("wmv", [4, 128, 8, 128])
    wuq_d = din("wuq", [128, 3, 1024])
    wuk_d = din("wuk", [128, 2, 1024])
    wuv_d = din("wuv", [128, 2, 512])
    wg_d = din("wg", [24, 128, 8, 128])
    wb_d = din("wb", [8, 3, 128, 4, 128])
    wout_d = din("wout", [128, 8, 1024])
    wup_d = din("wup", [128, 8, 4096])
    wdn_d = din("wdn", [128, 32, 1024])
    cols_d = din("cols", [128, NCOL])
    gpost_d = din("gpost", [2, 1024])
    cmat_d = din("cmat", [128, 7, 128])
    out_d = nc.dram_tensor("out", [NBC, S, D], F32, kind="ExternalOutput").ap()
    O_d = nc.dram_tensor("oscr", [NBC, 12, 128, S], BF16,
                         **({"kind": "ExternalOutput"} if DEBUG_SCRATCH else {})).ap()

    with ExitStack() as es:
        P = Prog(nc, es)
        env = Env()

        uniq = [0]

        def sbuf(scope, name, shape, dt):
            uniq[0] += 1
            nm = "t%d_%s" % (uniq[0], name)
            return Buf(scope.enter_context(nc.sbuf_tensor(nm, list(shape), dt)), nm)

        cols_b = sbuf(es, "cols", [128, NCOL], F32)
        cmat_b = sbuf(es, "cmat", [128, 7, 128], BF16)
        env.cols = cols_b.h
        env.cres = Res("consts")
        s_const = P.slot("const")
        P.dma("sp", cols_b.h[:], cols_d, s_const, writes=[env.cres])
        s_const2 = P.slot("const2")
        P.dma("pool", cmat_b.h[:], cmat_d, s_const2, writes=[env.cres])
        env.ident = cmat_b.h[:, 0, :]
        env.ones = cmat_b.h[:, 1, :]
        env.tneg = cmat_b.h[:, 2, :]
        env.negones = cmat_b.h[:, 3, :]
        env.mneg_sb = cmat_b.h[:, 4, :]
        env.mneg_mla = cmat_b.h[:, 5, :]
        env.f2 = cmat_b.h[:, 6, :]

        ps = es.enter_context(nc.psum_tensor("ps", [128, 8, 512], F32))
        env.ps = ps
        env.pst = ps[:, 6:8, :].bitcast(BF16)
        env.bank = [Res("bank%d" % i, excl=True) for i in range(8)]
        env.tbank = [env.bank[6], env.bank[7]]

        with ExitStack() as sa:
            phase_A(P, nc, sa, env, x_d, mem_d, pos_d, w1_d, w1v_d, wmk_d, wmv_d, wuq_d, wuk_d, wuv_d, O_d, sbuf)
            P.flush()
        if STOP_AFTER != "A":
            with ExitStack() as sc:
                phase_C(P, nc, sc, env, x_d, O_d, wg_d, wb_d, wout_d, gpost_d, out_d, sbuf)
                P.flush()
            if STOP_AFTER != "C":
                with ExitStack() as sd:
                    phase_D(P, nc, sd, env, wup_d, wdn_d, gpost_d, out_d, sbuf)
                    P.flush()
    return nc


def phase_A(P, nc, sa, env, x_d, mem_d, pos_d, w1_d, w1v_d, wmk_d, wmv_d, wuq_d, wuk_d, wuv_d, O_d, sbuf):
    ps = env.ps
    cols = env.cols
    sbk = sbuf(sa, "sbk", [128, 4, S], BF16)
    sbv = sbuf(sa, "sbv", [128, 16, 512], BF16)
    kaug = sbuf(sa, "kaug", [128, 8, S], BF16)
    vmla = sbuf(sa, "vmla", [128, 16, 512], BF16)
    vaug = [sbuf(sa, "vaug%d" % i, [128, 16, 128], BF16) for i in range(2)]
    memK = sbuf(sa, "memK", [128, 4, 256], BF16)
    memV = sbuf(sa, "memV", [128, 2, 512], BF16)
    wuq = sbuf(sa, "wuq", [128, 3, 1024], BF16)
    wuk = sbuf(sa, "wuk", [128, 2, 1024], BF16)
    wuv = sbuf(sa, "wuv", [128, 2, 512], BF16)
    xs = [sbuf(sa, "xs%d" % i, [128, 1024], F32) for i in range(2)]
    xs_slot = [P.slot("xs%d" % i) for i in range(2)]
    stage = dict(junk=None, ss=sbuf(sa, "ss", [128, 4], F32),
                 lnv=sbuf(sa, "lnv", [128, 4], F32), rstd=sbuf(sa, "rstd", [128, 4], F32),
                 hn=sbuf(sa, "hn", [128, 4, 1024], BF16))
    hT = sbuf(sa, "hT", [128, 8, TG], BF16)
    hT2 = sbuf(sa, "hT2", [128, 8, TG], BF16)
    w1v_slot = P.slot("w1v")
    wst = WStream(P, nc, sa, "wch", [128, 8, 128], 4)
    sq = [sbuf(sa, "sq%d" % i, [128, TG], BF16) for i in range(3)]
    cq = sbuf(sa, "cq", [128, 3, TG], BF16)
    ckv = sbuf(sa, "ckv", [128, 2, TG], BF16)
    yk = sbuf(sa, "yk", [128, TG], BF16)
    cs = sbuf(sa, "cs", [128, TG], F32)
    cs2 = sbuf(sa, "cs2", [128, TG], F32)
    pos_slot = P.slot("pos")
    sbq = sbuf(sa, "sbq", [128, 4, TG], BF16)
    qm = sbuf(sa, "qm", [128, 4, TG], BF16)
    qaug = sbuf(sa, "qaug", [128, 8, TG], BF16)
    omla = sbuf(sa, "omla", [128, 4, TG], BF16)
    ost_slot = [P.slot("ost%d" % i) for i in range(3)]
    ef = [sbuf(sa, "ef%d" % i, [128, 2, TG], F32) for i in range(2)]
    lsum = [sbuf(sa, "lsum%d" % i, [128, 2, TG], BF16) for i in range(2)]
    rec = [sbuf(sa, "rec%d" % i, [128, TG], F32) for i in range(2)]
    scrA = sbuf(sa, "scrA", [128, 5, 2, TG], BF16)

    class View:
        def __init__(self, h, res):
            self.h = h
            self._r = res

        def r(self, key=None):
            return self._r
    aT = [View(scrA.h[:, i], scrA.r(i)) for i in range(3)]
    stage["junk"] = View(scrA.h[:, 4].rearrange("p a t -> p (a t)"), scrA.r(4))
    lp = [View(scrA.h[:, 3 + i], scrA.r(3 + i)) for i in range(2)]
    w1vb_h = scrA.h[:, 0:4].rearrange("p a b t -> p (a b) t")
    alias_set = [scrA.r(i) for i in range(5)]
    ang_h, kf_h = ef[0].h[:, 0, :], ef[0].h[:, 1, :]
    posi_b = sbuf(sa, "posi", [128, TG], I32)
    posi_i = posi_b.h[:]
    ki_i = posi_b.h[:]
    ang = View(ang_h, ef[0].r())
    kf = View(kf_h, ef[0].r())
    posi = posi_b
    ki = posi_b

    lntmp, rq = rec[0], rec[1]

    class rkv:
        h = ef[1].h[:, 1, :]

        @staticmethod
        def r(key=None):
            return ef[1].r()
    s_w = P.slot("smallw")
    s_w2 = P.slot("smallw2")
    s_w3 = P.slot("smallw3")

    def load_small_weights():
        P.dma("pool", wuq.h[:], wuq_d, s_w, writes=[wuq.r()])
        P.dma("pool", wuk.h[:], wuk_d, s_w2, writes=[wuk.r()])
        P.dma("pool", wuv.h[:], wuv_d, s_w3, writes=[wuv.r()])
        for i in range(2):
            MSET(P, "pool", vaug[i].h[:, :, 64:128], 1.0, [vaug[i].r("ones")])

    banks = Rot([(ps[:, i, :], env.bank[i]) for i in range(6)])

    def load_x_tile(src_ap, i):
        P.dma("sp", xs[i].h[:], src_ap, xs_slot[i], writes=[xs[i].r()])

    pro_banks = Rot([(ps[:, 6 + i, :], env.bank[6 + i]) for i in range(2)])

    def project_chunks(srcs, consume, ncols, hT=hT, banks=banks):
        def body(j, h, r):
            bk, br = banks.get()
            for k in range(8):
                MM(P, bk[:, 0:ncols], h[:, k, :], hT.h[:, k, 0:ncols], k == 0, k == 7, [r, hT.r()], [br])
            consume(j, bk[:, 0:ncols], br)
        wst.run(srcs, body)

    def project_v(src_d, ntb, dst, tb0, hT=hT, banks=banks):
        P.dma("pool", w1vb_h, src_d, w1v_slot, writes=alias_set)
        for tb in range(ntb):
            bk, br = banks.get()
            for k in range(8):
                MM(P, bk, hT.h[:, k, tb * 128:(tb + 1) * 128], w1vb_h[:, k, :], k == 0, k == 7, [hT.r()] + alias_set, [br])
            if tb % 2 == 0:
                CP(P, "dve", dst.h[:, tb0 + tb, :], bk, [br], [dst.r(tb0 + tb)])
            else:
                ACT(P, dst.h[:, tb0 + tb, :], bk, AF.Copy, [br], [dst.r(tb0 + tb)])

    def mem_consume(j, bk, br):
        CP(P, "dve", memK.h[:, j, :], bk, [br], [memK.r()])

    def mem_chunks(b):
        def loads():
            for tb in range(2):
                load_x_tile(mem_d[b, tb * 128:(tb + 1) * 128, :], tb)
        nt = norm_transpose_chunks(P, env, [(xs[0].h[:], xs[0].r()), (xs[1].h[:], xs[1].r())], C_GMEM, hT.h, hT.r(),
                                   stage, [[0, 1]])
        def proj_v():
            bks = [pro_banks.get() for _ in range(2)]

            def body(c, h, r):
                for tb in range(2):
                    bk, br = bks[tb]
                    for k in range(8):
                        MM(P, bk[:, c * 128:(c + 1) * 128], hT.h[:, k, tb * 128:(tb + 1) * 128], h[:, k, :], k == 0, k == 7,
                           [hT.r(), r], [br], skip=True)
            wst.run([wmv_d[c] for c in range(4)], body)
            for tb in range(2):
                bk, br = bks[tb]
                CP(P, "dve", memV.h[:, tb, :], bk, [br], [memV.r(tb)])
        return [loads] + nt + [lambda: project_chunks([wmk_d[j] for j in range(4)], mem_consume, 256, hT=hT, banks=pro_banks),
                               proj_v]

    def rope_chunks(b, g, cs):
        t0 = g * TG

        def c0():
            P.dma("sp", posi_i, pos_d[b:b + 1, t0:t0 + TG].partition_broadcast(128), pos_slot, writes=[posi.r()])

        def c1():
            CP(P, "dve", kf.h, posi_i, [posi.r()], [kf.r()])
            TS(P, "dve", ang.h, kf.h, cols[:, C_INVF:C_INVF + 1], ALU.mult, [kf.r(), env.cres], [ang.r()],
               s2=cols[:, C_PHASE:C_PHASE + 1], op1=ALU.add)
            STT(P, ang.h, kf.h, cols[:, C_INVFLO:C_INVFLO + 1], ang.h, ALU.mult, ALU.add, [kf.r(), ang.r(), env.cres], [ang.r()])
            TS(P, "dve", kf.h, ang.h, 1.0 / (2 * math.pi), ALU.mult, [ang.r()], [kf.r()])
            CP(P, "dve", ki_i, kf.h, [kf.r()], [ki.r()])

        def c2():
            CP(P, "dve", kf.h, ki_i, [ki.r()], [kf.r()])
            C1 = 6.28125
            C2 = 2 * math.pi - C1
            STT(P, ang.h, kf.h, -C1, ang.h, ALU.mult, ALU.add, [kf.r(), ang.r()], [ang.r()])
            STT(P, ang.h, kf.h, -C2, ang.h, ALU.mult, ALU.add, [kf.r(), ang.r()], [ang.r()])
            TS(P, "dve", ang.h, ang.h, 3.1415925, ALU.min, [ang.r()], [ang.r()], s2=-3.1415925, op1=ALU.max)

        def c3():
            ACT(P, cs.h[:], ang.h, AF.Sin, [ang.r()], [cs.r()])
        return [c0, c1, c2, c3]

    def front_preload(b, g):
        for i in range(2):
            load_x_tile(x_d[b, g * TG + i * 128: g * TG + (i + 1) * 128, :], i)

    def front_chunks(b, g, hT):
        t0 = g * TG
        junk, ss, lnv, rstd, hn = stage["junk"], stage["ss"], stage["lnv"], stage["rstd"], stage["hn"]
        chunks = []
        for half in range(2):
            grp = [half * 2, half * 2 + 1]

            def stats(half=half, grp=grp):
                for i, tb in enumerate(grp):
                    ACT(P, junk.h[:], xs[i].h[:], AF.Square, [xs[i].r()], [junk.r(), ss.r(tb)], accum_out=ss.h[:, tb:tb + 1])
                a, bnd = grp[0], grp[-1] + 1
                ACT(P, lnv.h[:, a:bnd], ss.h[:, a:bnd], AF.Ln, [ss.r(t) for t in grp], [lnv.r(t) for t in grp],
                    scale=1.0 / D, bias=EPS)
                ACT(P, rstd.h[:, a:bnd], lnv.h[:, a:bnd], AF.Exp, [lnv.r(t) for t in grp], [rstd.r(t) for t in grp],
                    scale=-0.5)

            def scale_(half=half, grp=grp):
                for i, tb in enumerate(grp):
                    TS(P, "dve", hn.h[:, tb, :], xs[i].h[:], rstd.h[:, tb:tb + 1], ALU.mult, [xs[i].r(), rstd.r(tb)], [hn.r(tb)])
                if half == 0:
                    for i in range(2):
                        load_x_tile(x_d[b, t0 + (2 + i) * 128: t0 + (3 + i) * 128, :], i)
            chunks += [stats, scale_]
        for kp in range(4):
            def tr(kp=kp):
                bank = env.pst[:, kp % 2, :]
                bres = env.tbank[kp % 2]
                for kk in range(2):
                    k = 2 * kp + kk
                    for tb in range(4):
                        TR(P, bank[:, kk * 512 + tb * 128: kk * 512 + (tb + 1) * 128], hn.h[:, tb, k * 128:(k + 1) * 128],
                           env.ident, [hn.r(tb), env.cres], [bres])

            def ev(kp=kp):
                bank = env.pst[:, kp % 2, :]
                bres = env.tbank[kp % 2]
                for kk in range(2):
                    k = 2 * kp + kk
                    src = bank[:, kk * 512:(kk + 1) * 512]
                    gsc = cols[:, C_GPRE + k: C_GPRE + k + 1]
                    if kk == 0:
                        TS(P, "dve", hT.h[:, k, :], src, gsc, ALU.mult, [bres, env.cres], [hT.r()])
                    else:
                        ACT(P, hT.h[:, k, :], src, AF.Copy, [bres, env.cres], [hT.r()], scale=gsc)
            chunks += [tr, ev]
        return chunks

    hT_bufs = [hT, hT2]
    cs_bufs = [cs, cs2]

    def prologue_chunks(b):
        ch = [load_small_weights] if b == 0 else []
        ch += mem_chunks(b)
        ch += [lambda: front_preload(b, 0)]
        ch += rope_chunks(b, 0, cs_bufs[0]) + front_chunks(b, 0, hT_bufs[0])
        return ch

    for ch in prologue_chunks(0):
        ch()
    for b, g in [(b_, g_) for b_ in range(NBC) for g_ in range(NG)]:
        t0 = g * TG
        hT = hT_bufs[g % 2]
        cs = cs_bufs[g % 2]

        def consume(c, bk, br):
            if c < 3:
                CP(P, "dve", cq.h[:, c, :], bk, [br], [cq.r(c)])
                ACT(P, sq[c % 3].h[:], bk, AF.Square, [br], [sq[c % 3].r()])
                if c == 2:
                    sb_, sr_ = banks.get()
                    for j in range(3):
                        MM(P, sb_, env.ones, sq[j].h[:], j == 0, j == 2, [sq[j].r(), env.cres], [sr_])
                    rstd_from_psum(P, env, sb_, sr_, 384, lntmp, rq, slice(0, TG))
                    for j in range(3):
                        STT(P, cq.h[:, j, :], cq.h[:, j, :], cols[:, C_GQ + j:C_GQ + j + 1], rq.h[:], ALU.mult, ALU.mult,
                            [cq.r(j), rq.r(), env.cres], [cq.r(j)])
            elif c < 5:
                j = c - 3
                CP(P, "dve", ckv.h[:, j, :], bk, [br], [ckv.r(j)])
                ACT(P, sq[j].h[:], bk, AF.Square, [br], [sq[j].r()])
                if j == 1:
                    sb_, sr_ = banks.get()
                    for jj in range(2):
                        MM(P, sb_, env.ones, sq[jj].h[:], jj == 0, jj == 1, [sq[jj].r(), env.cres], [sr_])
                    rstd_from_psum(P, env, sb_, sr_, 256, lntmp, rkv, slice(0, TG))
                    for jj in range(2):
                        STT(P, ckv.h[:, jj, :], ckv.h[:, jj, :], cols[:, C_GKV + jj:C_GKV + jj + 1], rkv.h[:], ALU.mult,
                            ALU.mult, [ckv.r(jj), rkv.r(), env.cres], [ckv.r(jj)])
            elif c == 5:
                TT(P, "dve", yk.h[:], bk, cs.h[:], ALU.mult, [br, cs.r()], [yk.r()])
            elif c < 10:
                j = c - 6
                ACT(P, sbq.h[:, j, :], bk, AF.Copy, [br], [sbq.r((j, 0)), sbq.r((j, 1))], scale=0.125)
            elif c < 14:
                j = c - 10
                CP(P, "dve", sbk.h[:, j, t0:t0 + TG], bk, [br], [sbk.r((j, g))])
            else:
                j = c - 14
                ACT(P, qm.h[:, j, :], bk, AF.Copy, [br], [qm.r(j)])
        rope_next = rope_chunks(b, g + 1, cs_bufs[(g + 1) % 2]) if g + 1 < NG else []
        if rope_next:
            rope_next[0]()
        project_chunks([w1_d[c] for c in range(18)], consume, TG, hT=hT)
        for ch in rope_next[1:]:
            ch()
        project_v(w1v_d, 4, sbv, g * 4, hT=hT)

        for h in range(8):
            bk, br = banks.get()
            for k in range(3):
                MM(P, bk, wuq.h[:, k, h * 128:(h + 1) * 128], cq.h[:, k, :], k == 0, k == 2, [wuq.r(), cq.r(k)], [br])
            TT(P, "dve", qaug.h[:, h, :], bk, cs.h[:], ALU.mult, [br, cs.r()], [qaug.r(h)])
        for h in range(8):
            bk, br = banks.get()
            for k in range(2):
                MM(P, bk, wuk.h[:, k, h * 128:(h + 1) * 128], ckv.h[:, k, :], k == 0, False, [wuk.r(), ckv.r(k)], [br])
            MM(P, bk, env.f2, yk.h[:], False, True, [yk.r(), env.cres], [br])
            ACT(P, kaug.h[:, h, t0:t0 + TG], bk, AF.Copy, [br], [kaug.r((h, g))])
        for tb in range(4):
            bk, br = banks.get()
            for k in range(2):
                MM(P, bk, ckv.h[:, k, tb * 128:(tb + 1) * 128], wuv.h[:, k, :], k == 0, k == 1, [ckv.r(k), wuv.r()], [br])
            CP(P, "dve", vmla.h[:, g * 4 + tb, :], bk, [br], [vmla.r(g * 4 + tb)])

        nkb = 4 * g + 4
        abanks = Rot([(ps[:, i, :], env.bank[i]) for i in range(8)])
        pbanks = Rot([(ps[:, 2 * i:2 * i + 2, :], [env.bank[2 * i], env.bank[2 * i + 1]]) for i in range(3)])
        obanks = Rot([(ps[:, 6 + i, :], env.bank[6 + i]) for i in range(2)])
        mla_s = Rot([(ps[:, i, :], env.bank[i]) for i in range(4)])
        mla_o = Rot([(ps[:, 4 + i, :], env.bank[4 + i]) for i in range(2)])

        MSCALE = 128 ** -0.5
        mtasks = []
        for h in range(4):
            mtasks.append(mem_task(P, env, h, MSCALE, memK, memV, qm, aT[h % 2], rec[h % 2], ef[h % 2],
                                   [abanks.get() for _ in range(4)]))
        run_pipeline(mtasks, [("Z", 1), ("PV", 0), ("N", -1)])
        P.dma("sp", O_d[b, 8:12, :, t0:t0 + TG].rearrange("c p t -> p c t"), qm.h[:], ost_slot[2],
              reads=[qm.r(j) for j in range(4)])

        if g + 1 < NG:
            front_preload(b, g + 1)
        tasks = []
        ti = 0
        for hp in range(4):
            ob, obr = obanks.get()
            ls = lsum[hp % 2]
            for idx, kb in enumerate(range(nkb - 1, -1, -1)):
                tasks.append(sb_task(P, env, g, hp, idx, kb, nkb, ti, ob, obr, ls, sbq, sbk, sbv, pbanks.get(), ef, lp, aT))
                ti += 1
        run_pipeline(tasks, [("Z", 1), ("A", -1), ("L", 0), ("E", 1), ("CUMA", 0), ("AV", -1), ("CUMB", 0)])
        P.dma("sp", O_d[b, 4:8, :, t0:t0 + TG].rearrange("c p t -> p c t"), sbq.h[:], ost_slot[1],
              reads=[sbq.r((j, i)) for j in range(4) for i in range(2)])

        extras = []
        if g + 1 < NG:
            extras = front_chunks(b, g + 1, hT_bufs[(g + 1) % 2])
        elif b + 1 < NBC:
            extras = prologue_chunks(b + 1)

        tasks = []
        ti = 0
        for h in range(8):
            ob, obr = mla_o.get()
            for kb in range(nkb):
                tasks.append(mla_task(P, env, g, h, kb, nkb, ti, ob, obr, kaug, qaug, vmla, vaug[h % 2], omla, rec[h % 2],
                                      mla_s.get(), aT, lt=ef[1] if (g <= 1 and h % 2 == 1) else None))
                ti += 1
        run_pipeline(tasks, [("Z", 1), ("P", 0), ("PV", -1)], extras=extras,
                     every=max(1, len(tasks) // (len(extras) + 1)))
        P.dma("sp", O_d[b, 0:4, :, t0:t0 + TG].rearrange("c p t -> p c t"), omla.h[:], ost_slot[0],
              reads=[omla.r((j, i)) for j in range(4) for i in range(2)])


def mem_task(P, env, h, scale, memK, memV, qm, a2, rc, lt, bks):
    (zb0, zr0), (zb1, zr1), (ob, obr), (db, dbr) = bks
    zs = [(zb0, zr0), (zb1, zr1)]

    def fZ():
        for mt in range(2):
            MM(P, zs[mt][0], memK.h[:, h, mt * 128:(mt + 1) * 128], qm.h[:, h, :], True, True, [memK.r(), qm.r(h)], [zs[mt][1]])
        for mt in range(2):
            ACT(P, a2.h[:, mt, :], zs[mt][0], AF.Exp, [zs[mt][1]], [a2.r()], scale=scale)

    def fPV():
        for mt in range(2):
            MM(P, ob, memV.h[:, mt, h * 128:(h + 1) * 128], a2.h[:, mt, :], mt == 0, mt == 1, [memV.r(mt), a2.r()], [obr])
        for mt in range(2):
            MM(P, db, env.ones, a2.h[:, mt, :], mt == 0, mt == 1, [a2.r(), env.cres], [dbr])

    def fN():
        ACT(P, lt.h[:, 0, :], db, AF.Ln, [dbr], [lt.r()])
        ACT(P, rc.h[:], lt.h[:, 0, :], AF.Exp, [lt.r()], [rc.r()], scale=-1.0)
        TT(P, "dve", qm.h[:, h, :], ob, rc.h[:], ALU.mult, [obr, rc.r()], [qm.r(h)])

    return {"Z": fZ, "PV": fPV, "N": fN}


def run_pipeline(tasks, order, extras=(), every=3):
    n = len(tasks)
    lo = min(off for _, off in order)
    hi = max(off for _, off in order)
    extras = list(extras)
    for cnt, s_ in enumerate(range(-hi, n - lo)):
        for name, off in order:
            i = s_ + off
            if 0 <= i < n:
                tasks[i][name]()
        if extras and cnt % every == every - 1:
            extras.pop(0)()
    for ex in extras:
        ex()


def sb_task(P, env, g, hp, idx, kb, nkb, ti, ob, obr, ls, sbq, sbk, sbv, zbank, ef, lp, aT):
    j = hp
    jd = kb - 4 * g
    c0 = 128 * jd if jd >= 0 else 0
    kres = sbk.r((j, kb // 4))
    zb2, zr = zbank
    e_, l_, a = ef[ti % 2], lp[ti % 2], aT[ti % 3]

    def fZ():
        for i in range(2):
            po = 64 * i
            MM(P, zb2[:, i, c0:TG], sbk.h[po:po + 64, j, kb * 128:(kb + 1) * 128], sbq.h[po:po + 64, j, c0:TG], True, jd < 0,
               [kres, sbq.r((j, i))], [zr[i]])
        if jd >= 0:
            for i in range(2):
                MM(P, zb2[:, i, c0:c0 + 128], env.ident, env.mneg_sb, False, True, [env.cres], [zr[i]])

    def fE():
        ACT(P, e_.h[:, :, c0:TG], zb2[:, :, c0:TG], AF.Exp, zr, [e_.r()])

    def fL():
        ACT(P, l_.h[:, :, c0:TG], e_.h[:, :, c0:TG], AF.Ln, [e_.r()], [l_.r()], bias=1.0)

    def fCUMA():
        if idx == 0:
            MSET(P, "pool", ls.h[:], 0.0, [ls.r()])
        if idx > 0:
            c1 = 128 * (jd + 1) if jd >= 0 else 0
            for i in range(2):
                MM(P, zb2[:, i, c1:TG], env.negones, ls.h[:, i, c1:TG], False, True, [ls.r(), env.cres], [zr[i]], skip=True)

    def fCUMB():
        for i in range(2):
            MM(P, zb2[:, i, c0:TG], env.tneg, l_.h[:, i, c0:TG], False, True, [l_.r(), env.cres], [zr[i]], skip=True)
        if kb > 0:
            TT(P, "dve", ls.h[:, :, c0:TG], ls.h[:, :, c0:TG], l_.h[:, :, c0:TG], ALU.add, [ls.r(), l_.r()], [ls.r()])

    def fA():
        ACT(P, a.h[:, :, c0:TG], zb2[:, :, c0:TG], AF.Exp, zr, [a.r()])

    def fAV():
        for i in range(2):
            po = 64 * i
            h = 2 * hp + i
            MM(P, ob[po:po + 64, c0:TG], sbv.h[:, kb, h * 64:(h + 1) * 64], a.h[:, i, c0:TG], idx == 0, kb == 0,
               [sbv.r(kb), a.r()], [obr], skip=True)
        if kb == 0:
            CP(P, "dve", sbq.h[:, j, :], ob, [obr], [sbq.r((j, 0)), sbq.r((j, 1))])

    return {"Z": fZ, "E": fE, "L": fL, "CUMA": fCUMA, "CUMB": fCUMB, "A": fA, "AV": fAV}


def mla_task(P, env, g, h, kb, nkb, ti, ob, obr, kaug, qaug, vmla, va, omla, rc, zbank, aT, lt=None):
    ASCALE = 96 ** -0.5
    j, po = h // 2, (h % 2) * 64
    jd = kb - 4 * g
    c0 = 128 * jd if jd >= 0 else 0
    zb, zr = zbank
    a3 = aT[ti % 3]
    a_res = a3.r()

    class a:
        h = a3.h[:, 0, :]

        @staticmethod
        def r():
            return a_res

    def fZ():
        if kb == 0:
            CP(P, "pool", va.h[:, 0:nkb, 0:64], vmla.h[:, 0:nkb, h * 64:(h + 1) * 64],
               [vmla.r(t) for t in range(nkb)], [va.r("v")])
        MM(P, zb[:, c0:TG], kaug.h[:, h, kb * 128:(kb + 1) * 128], qaug.h[:, h, c0:TG], True, jd < 0,
           [kaug.r((h, kb // 4)), qaug.r(h)], [zr])
        if jd >= 0:
            MM(P, zb[:, c0:c0 + 128], env.ident, env.mneg_mla, False, True, [env.cres], [zr])

    def fP():
        ACT(P, a.h[:, c0:TG], zb[:, c0:TG], AF.Exp, [zr], [a.r()], scale=ASCALE)

    def fPV():
        MM(P, ob[:, c0:TG], va.h[:, kb, :], a.h[:, c0:TG], kb == 0, kb == nkb - 1,
           [va.r("v"), va.r("ones"), a.r()], [obr], skip=True)
        if kb == nkb - 1:
            if lt is not None:
                ACT(P, lt.h[0:64, 0, :], ob[64:128, :], AF.Ln, [obr], [lt.r()])
                ACT(P, rc.h[0:64, :], lt.h[0:64, 0, :], AF.Exp, [lt.r()], [rc.r()], scale=-1.0)
            else:
                RECIP(P, rc.h[0:64, :], ob[64:128, :], [obr], [rc.r()])
            TT(P, "dve", omla.h[po:po + 64, j, :], ob[0:64, :], rc.h[0:64, :], ALU.mult, [obr, rc.r()],
               [omla.r((j, h % 2))])

    return {"Z": fZ, "P": fP, "PV": fPV}


def phase_C(P, nc, sc, env, x_d, O_d, wg_d, wb_d, wout_d, gpost_d, out_d, sbuf):
    ps = env.ps
    cols = env.cols
    wg = sbuf(sc, "wg", [128, 24, 8, 128], BF16)
    wb = sbuf(sc, "wbr", [128, 24, 4, 128], BF16)
    wout = sbuf(sc, "wout", [128, 8, 1024], BF16)
    gbc = sbuf(sc, "gbc", [128, 1024], F32)
    og = [sbuf(sc, "og%d" % i, [128, 12, TG], BF16) for i in range(2)]
    og_slot = [P.slot("og%d" % i) for i in range(2)]
    xr = [sbuf(sc, "xr%d" % i, [128, 4, 1024], F32) for i in range(2)]
    xr_slot = [P.slot("xr%d" % i) for i in range(2)]
    st_slot = [P.slot("stc%d" % i) for i in range(2)]
    stage = dict(junk=sbuf(sc, "junkc", [128, 1024], BF16), ss=sbuf(sc, "ssc", [128, 4], F32),
                 lnv=sbuf(sc, "lnvc", [128, 4], F32), rstd=sbuf(sc, "rstdc", [128, 4], F32),
                 hn=sbuf(sc, "hnc", [128, 4, 1024], BF16))
    hT = sbuf(sc, "hTc", [128, 8, TG], BF16)
    merged = sbuf(sc, "merged", [128, 8, TG], BF16)
    gt = [[sbuf(sc, "gt%d_%d" % (r, i), [128, TG], F32) for i in range(3)] for r in range(2)]
    ssy = sbuf(sc, "ssy", [128, 4], F32)
    lny = sbuf(sc, "lny", [128, 4], F32)
    rsy = sbuf(sc, "rsy", [128, 4], F32)
    tt = [sbuf(sc, "tt%d" % i, [128, 1024], F32) for i in range(2)]

    wslots = [P.slot("wc%d" % i) for i in range(17)]
    def load_weights(first_dep):
        for e in range(8):
            P.dma("pool", wg.h[:, e * 3:(e + 1) * 3, :, :], wg_d[e * 3:(e + 1) * 3].rearrange("c p k m -> p c k m"),
                  wslots[e], reads=first_dep if e == 0 else (), writes=[wg.r(e)])
            P.dma("pool", wb.h[:, e * 3:(e + 1) * 3, :, :], wb_d[e].rearrange("i p k m -> p i k m"), wslots[8 + e],
                  writes=[wb.r(e)])
            if e == 1:
                P.dma("pool", wout.h[:], wout_d, wslots[16], writes=[wout.r()])
    s_g = P.slot("gbc")

    banks = Rot([(ps[:, i, :], env.bank[i]) for i in range(6)])
    ybanks = Rot([(ps[:, 2 * i:2 * i + 2, :], (env.bank[2 * i], env.bank[2 * i + 1])) for i in range(3)])
    groups = [(b, g) for b in range(NBC) for g in range(NG)]

    def issue_loads(n):
        b, g = groups[n]
        t0 = g * TG
        P.dma("sp", og[n % 2].h[:], O_d[b, :, :, t0:t0 + TG].rearrange("c p t -> p c t"), og_slot[n % 2],
              writes=[og[n % 2].r()])
        P.dma("sp", xr[n % 2].h[:], x_d[b, t0:t0 + TG, :].rearrange("(t p) d -> p t d", p=128), xr_slot[n % 2],
              writes=[xr[n % 2].r(t) for t in range(4)])

    def prep(n):
        xb_ = xr[n % 2]
        return norm_transpose_chunks(P, env, [(xb_.h[:, t, :], xb_.r(t)) for t in range(4)], C_GPRE, hT.h, hT.r(),
                                     stage, [[0, 1, 2, 3]])

    issue_loads(0)
    P.dma("sp", gbc.h[:], gpost_d[0:1, :].partition_broadcast(128), s_g, writes=[gbc.r()])
    load_weights([xr[0].r(t) for t in range(4)])
    for n, (b, g) in enumerate(groups):
        t0 = g * TG
        if n + 1 < len(groups):
            issue_loads(n + 1)
        xb = xr[n % 2]
        ogb = og[n % 2]
        if n == 0:
            for ch in prep(0):
                ch()
        pchunks = prep(n + 1) if n + 1 < len(groups) else []
        for e in range(8):
            if e == 3 and pchunks:
                pchunks[0]()
            gts = gt[e % 2]
            for i in range(3):
                bk, br = banks.get()
                c = i * 8 + e
                for k in range(8):
                    MM(P, bk, wg.h[:, e * 3 + i, k, :], hT.h[:, k, :], k == 0, k == 7, [wg.r(e), hT.r()], [br])
                ACT(P, gts[i].h[:], bk, AF.Sigmoid, [br, env.cres], [gts[i].r()], bias=cols[:, C_BG + c:C_BG + c + 1])
                bk2, br2 = banks.get()
                for k in range(4):
                    MM(P, bk2, wb.h[:, e * 3 + i, k, :], ogb.h[:, i * 4 + k, :], k == 0, k == 3, [wb.r(e), ogb.r()], [br2])
                TT(P, "dve", gts[i].h[:], bk2, gts[i].h[:], ALU.mult, [br2, gts[i].r()], [gts[i].r()])
            TT(P, "pool", gts[0].h[:], gts[0].h[:], gts[1].h[:], ALU.add, [gts[0].r(), gts[1].r()], [gts[0].r()])
            TT(P, "pool", merged.h[:, e, :], gts[0].h[:], gts[2].h[:], ALU.add, [gts[0].r(), gts[2].r()], [merged.r()])
        for ch in pchunks[1:]:
            ch()
        for tb in range(4):
            yb, (yr0, yr1) = ybanks.get()
            for half in range(2):
                for k in range(8):
                    MM(P, yb[:, half, :], merged.h[:, k, tb * 128:(tb + 1) * 128], wout.h[:, k, half * 512:(half + 1) * 512],
                       k == 0, k == 7, [merged.r(), wout.r()], [yr0 if half == 0 else yr1])
            ACT(P, stage["junk"].h[:].rearrange("p (a b) -> p a b", a=2), yb, AF.Square, [yr0, yr1], [stage["junk"].r(), ssy.r(tb)],
                accum_out=ssy.h[:, tb:tb + 1])
            ACT(P, lny.h[:, tb:tb + 1], ssy.h[:, tb:tb + 1], AF.Ln, [ssy.r(tb)], [lny.r(tb)], scale=1.0 / D, bias=EPS)
            ACT(P, rsy.h[:, tb:tb + 1], lny.h[:, tb:tb + 1], AF.Exp, [lny.r(tb)], [rsy.r(tb)], scale=-0.5)
            t_ = tt[tb % 2]
            for half in range(2):
                STT(P, t_.h[:, half * 512:(half + 1) * 512], yb[:, half, :], rsy.h[:, tb:tb + 1],
                    gbc.h[:, half * 512:(half + 1) * 512], ALU.mult, ALU.mult,
                    [yr0 if half == 0 else yr1, rsy.r(tb), gbc.r()], [t_.r()])
            TT(P, "pool", xb.h[:, tb, :], xb.h[:, tb, :], t_.h[:], ALU.add, [xb.r(tb), t_.r()], [xb.r(tb)])
        P.dma("sp", out_d[b, t0:t0 + TG, :].rearrange("(t p) d -> p t d", p=128), xb.h[:], st_slot[n % 2],
              reads=[xb.r(t) for t in range(4)])


def phase_D(P, nc, sd, env, wup_d, wdn_d, gpost_d, out_d, sbuf):
    ps = env.ps
    wup = sbuf(sd, "wup", [128, 8, 4096], BF16)
    wdn = sbuf(sd, "wdn", [128, 32, 1024], BF16)
    gbc = sbuf(sd, "gbd", [128, 1024], F32)
    UT = 256
    x1 = [sbuf(sd, "x1_%d" % i, [128, 2, 1024], F32) for i in range(2)]
    x1_slot = [P.slot("x1_%d" % i) for i in range(2)]
    st_slot = [P.slot("std%d" % i) for i in range(2)]
    stage = dict(junk=sbuf(sd, "junkd", [128, 1024], BF16), ss=sbuf(sd, "ssd", [128, 4], F32),
                 lnv=sbuf(sd, "lnvd", [128, 4], F32), rstd=sbuf(sd, "rstdd", [128, 4], F32),
                 hn=sbuf(sd, "hnd", [128, 2, 1024], BF16))
    h2T = [sbuf(sd, "h2T%d" % i, [128, 8, UT], BF16) for i in range(2)]
    rl = [sbuf(sd, "rl%d" % i, [128, UT], BF16) for i in range(3)]
    uT = [sbuf(sd, "uT%d" % i, [128, UT], BF16) for i in range(8)]
    ssz = sbuf(sd, "ssz", [128, 4], F32)
    ssz2 = sbuf(sd, "ssz2", [128, 2], F32)
    lnz = sbuf(sd, "lnz", [128, 2], F32)
    rsz = sbuf(sd, "rsz", [128, 2], F32)
    tt = [sbuf(sd, "ttd%d" % i, [128, 1024], F32) for i in range(2)]

    wslots = [P.slot("wd%d" % i) for i in range(16)]
    def load_weights(first_dep):
        for q in range(8):
            P.dma("pool", wup.h[:, :, q * 512:(q + 1) * 512], wup_d[:, :, q * 512:(q + 1) * 512], wslots[q],
                  reads=first_dep if q == 0 else (), writes=[wup.r(q)])
            P.dma("pool", wdn.h[:, q * 4:(q + 1) * 4, :], wdn_d[:, q * 4:(q + 1) * 4, :], wslots[8 + q], writes=[wdn.r(q)])
    s_g = P.slot("gbd")

    zbank = [(ps[:, i, :], env.bank[i]) for i in range(4)]
    ubanks = Rot([(ps[:, 4 + i, :], env.bank[4 + i]) for i in range(2)])
    units = [(b, u) for b in range(NBC) for u in range(S // UT)]

    def issue_load(n):
        b, u = units[n]
        t0 = u * UT
        P.dma("sp", x1[n % 2].h[:], out_d[b, t0:t0 + UT, :].rearrange("(t p) d -> p t d", p=128), x1_slot[n % 2],
              writes=[x1[n % 2].r(t) for t in range(2)])

    def prep(n):
        xb_ = x1[n % 2]
        return norm_transpose_chunks(P, env, [(xb_.h[:, t, :], xb_.r(t)) for t in range(2)], C_GMLP, h2T[n % 2].h,
                                     h2T[n % 2].r(), stage, [[0, 1]])

    DEPTH = 3
    NU = DEPTH + 2
    issue_load(0)
    P.dma("sp", gbc.h[:], gpost_d[1:2, :].partition_broadcast(128), s_g, writes=[gbc.r()])
    load_weights([x1[0].r(t) for t in range(2)])
    for ch in prep(0):
        ch()
    for n, (b, u) in enumerate(units):
        t0 = u * UT
        if n + 1 < len(units):
            issue_load(n + 1)
        xb = x1[n % 2]
        hcur = h2T[n % 2]

        def up(f):
            ub, ur = ubanks.get()
            for k in range(8):
                MM(P, ub[:, 0:UT], wup.h[:, k, f * 128:(f + 1) * 128], hcur.h[:, k, :], k == 0, k == 7,
                   [wup.r(f // 4), hcur.r()], [ur])
            r_ = rl[f % 3]
            u_ = uT[f % NU]
            ACT(P, r_.h[:], ub[:, 0:UT], AF.Relu, [ur], [r_.r()])
            TT(P, "dve", u_.h[:], r_.h[:], r_.h[:], ALU.mult, [r_.r()], [u_.r()])

        def down(f):
            u_ = uT[f % NU]
            for tb in range(2):
                for half in range(2):
                    zb, zr = zbank[tb * 2 + half]
                    MM(P, zb, u_.h[:, tb * 128:(tb + 1) * 128], wdn.h[:, f, half * 512:(half + 1) * 512], f == 0, f == 31,
                       [u_.r(), wdn.r(f // 4)], [zr])

        pchunks = prep(n + 1) if n + 1 < len(units) else []
        sched = {8: 0, 18: 1, 21: 2, 24: 3, 27: 4}
        for f in range(32 + DEPTH):
            if f < 32:
                up(f)
            if f in sched and pchunks:
                pchunks[sched[f]]()
            if f - DEPTH >= 0:
                down(f - DEPTH)
        for tb in range(2):
            for half in range(2):
                zb, zr = zbank[tb * 2 + half]
                ACT(P, stage["junk"].h[:, 0:512], zb, AF.Square, [zr], [stage["junk"].r(), ssz.r(tb * 2 + half)],
                    accum_out=ssz.h[:, tb * 2 + half: tb * 2 + half + 1])
            TT(P, "dve", ssz2.h[:, tb:tb + 1], ssz.h[:, 2 * tb:2 * tb + 1], ssz.h[:, 2 * tb + 1:2 * tb + 2], ALU.add,
               [ssz.r(2 * tb), ssz.r(2 * tb + 1)], [ssz2.r(tb)])
            ACT(P, lnz.h[:, tb:tb + 1], ssz2.h[:, tb:tb + 1], AF.Ln, [ssz2.r(tb)], [lnz.r(tb)], scale=1.0 / D, bias=EPS)
            ACT(P, rsz.h[:, tb:tb + 1], lnz.h[:, tb:tb + 1], AF.Exp, [lnz.r(tb)], [rsz.r(tb)], scale=-0.5)
            t_ = tt[tb % 2]
            for half in range(2):
                zb, zr = zbank[tb * 2 + half]
                STT(P, t_.h[:, half * 512:(half + 1) * 512], zb, rsz.h[:, tb:tb + 1], gbc.h[:, half * 512:(half + 1) * 512],
                    ALU.mult, ALU.mult, [zr, rsz.r(tb), gbc.r()], [t_.r()])
            TT(P, "pool", xb.h[:, tb, :], xb.h[:, tb, :], t_.h[:], ALU.add, [xb.r(tb), t_.r()], [xb.r(tb)])
        P.dma("sp", out_d[b, t0:t0 + UT, :].rearrange("(t p) d -> p t d", p=128), xb.h[:], st_slot[n % 2],
              reads=[xb.r(t) for t in range(2)])


def _chunkify(W):
    C = W.shape[1]
    return np.ascontiguousarray(W.reshape(8, 128, C // 128, 128).transpose(2, 1, 0, 3))


def _rows(W, nk):
    return np.ascontiguousarray(W.reshape(nk, 128, W.shape[1]).transpose(1, 0, 2))


def prep_shared(inp):
    f = np.float32
    w_in = np.asarray(inp["w_in"], f)[0]
    kr = np.zeros((D, 128), f)
    kr[:, 64:96] = w_in[:, 640:672]
    kr[:, 96:112] = w_in[:, 656:672]
    kr[:, 112:128] = w_in[:, 640:656]
    w1cat = np.concatenate([w_in[:, 0:640], kr, w_in[:, 672:1184], w_in[:, 1184:1696], w_in[:, 2208:2720]], axis=1)
    sh = {}
    sh["w1"] = _chunkify(w1cat)
    sh["w1v"] = _rows(w_in[:, 1696:2208], 8)
    wmkv = np.asarray(inp["w_mem_kv"], f)[0]
    sh["wmk"] = _chunkify(wmkv[:, 0:512])
    sh["wmv"] = _chunkify(wmkv[:, 512:1024])
    w_uq = np.asarray(inp["w_uq"], f)[0]
    wq = np.zeros((384, 1024), f)
    for h in range(8):
        s = h * 96
        wq[:, h * 128:h * 128 + 64] = w_uq[:, s:s + 64]
        wq[:, h * 128 + 64:h * 128 + 96] = w_uq[:, s + 64:s + 96]
        wq[:, h * 128 + 96:h * 128 + 112] = w_uq[:, s + 80:s + 96]
        wq[:, h * 128 + 112:h * 128 + 128] = w_uq[:, s + 64:s + 80]
    sh["wuq"] = _rows(wq, 3)
    w_uk = np.asarray(inp["w_uk"], f)[0]
    wk = np.zeros((256, 1024), f)
    for h in range(8):
        wk[:, h * 128:h * 128 + 64] = w_uk[:, h * 64:(h + 1) * 64]
    sh["wuk"] = _rows(wk, 2)
    sh["wuv"] = _rows(np.asarray(inp["w_uv"], f)[0], 2)
    wgc = _chunkify(w_in[:, 2720:5792])
    sh["wg"] = np.ascontiguousarray(wgc.reshape(3, 8, 128, 8, 128).transpose(1, 0, 2, 3, 4).reshape(24, 128, 8, 128))
    wbo = np.asarray(inp["w_branch_out"], f)[0]
    sh["wb"] = np.ascontiguousarray(wbo.reshape(3, 4, 128, 8, 128).transpose(3, 0, 2, 1, 4))
    sh["wout"] = _rows(np.asarray(inp["w_out"], f)[0], 8)
    sh["wup"] = _rows(np.asarray(inp["w_mlp_up"], f)[0], 8)
    sh["wdn"] = _rows(np.asarray(inp["w_mlp_down"], f)[0], 32)
    cols = np.zeros((128, NCOL), f)
    cols[:, C_GPRE:C_GPRE + 8] = np.asarray(inp["ln_mix_pre"], f)[0].reshape(8, 128).T
    cols[:, C_GMEM:C_GMEM + 8] = np.asarray(inp["mem_norm"], f)[0].reshape(8, 128).T
    cols[:, C_GMLP:C_GMLP + 8] = np.asarray(inp["ln_mlp_pre"], f)[0].reshape(8, 128).T
    cols[:, C_GQ:C_GQ + 3] = np.asarray(inp["q_norm"], f)[0].reshape(3, 128).T
    cols[:, C_GKV:C_GKV + 2] = np.asarray(inp["kv_norm"], f)[0].reshape(2, 128).T
    cols[:, C_BG:C_BG + 24] = np.asarray(inp["b_gate"], f)[0].reshape(24, 128).T
    invf64 = 1.0 / (10000.0 ** (np.arange(16, dtype=np.float64) * (2.0 / 32)))
    invf = invf64.astype(f)
    invf_lo = (invf64 - invf.astype(np.float64)).astype(f)
    ivl = np.zeros(128, f)
    ivl[64:80] = invf_lo
    ivl[80:96] = invf_lo
    ivl[96:112] = -invf_lo
    ivl[112:128] = invf_lo
    cols[:, C_INVFLO] = ivl
    iv = np.zeros(128, f)
    ph = np.zeros(128, f)
    ph[0:96] = np.pi / 2
    iv[64:80] = invf
    iv[80:96] = invf
    iv[96:112] = -invf
    iv[112:128] = invf
    cols[:, C_INVF] = iv
    cols[:, C_PHASE] = ph
    sh["cols"] = cols
    sh["gpost"] = np.stack([np.asarray(inp["ln_mix_post"], f)[0], np.asarray(inp["ln_mlp_post"], f)[0]])
    jj = np.arange(128)[:, None]
    tt = np.arange(128)[None, :]
    cm = np.zeros((7, 128, 128), f)
    cm[0] = np.eye(128, dtype=f)
    cm[1] = 1.0
    cm[2] = np.where(jj >= tt, -1.0, 0.0)
    cm[3] = -1.0
    cm[4] = np.where(tt <= jj, MASKV, 0.0)
    cm[5] = np.where(tt < jj, MASKV, 0.0)
    for j in range(64):
        cm[6][64 + j, 64 + j % 32] = 1.0
        cm[6][64 + j, 96 + j % 32] = 1.0
    sh["cmat"] = np.ascontiguousarray(cm.transpose(1, 0, 2))
    return sh


_NC_CACHE = {}


def kernel(**inputs):
    x = np.ascontiguousarray(np.asarray(inputs["x"], np.float32))
    mem = np.ascontiguousarray(np.asarray(inputs["mem"], np.float32))
    pos = np.ascontiguousarray(np.asarray(inputs["positions"], np.int32))
    sh = prep_shared(inputs)
    if "nc" not in _NC_CACHE:
        _NC_CACHE["nc"] = build_program()
    nc = _NC_CACHE["nc"]
    in_maps = []
    for c in range(NCORES):
        m = dict(sh)
        m["x"] = x[c * NBC:(c + 1) * NBC]
        m["mem"] = mem[c * NBC:(c + 1) * NBC]
        m["pos"] = pos[c * NBC:(c + 1) * NBC]
        in_maps.append(m)
    res = run_bass_kernel_spmd(nc, in_maps, core_ids=list(range(NCORES)))
    kernel.last_results = res
    return np.concatenate([np.asarray(r["out"], np.float32) for r in res.results], axis=0)
```
